# Optimizing a Trainium2 kernel written in Bass

```python
import jax
import jax.numpy as jnp
from jax import lax
import numpy as np


D_MODEL = 1024
BATCH = 8
SEQ = 2048
DEPTH = 4

CHUNK = 64
HEAD_DIM = 64
D_RWKV = 512
D_ATT = 512
D_MIX = D_RWKV + D_ATT
N_RWKV_HEADS = D_RWKV // HEAD_DIM
N_ATT_HEADS = D_ATT // HEAD_DIM
LORA_W = 64
LORA_A = 64
N_LEFT_CHUNKS = 8
BAND = (N_LEFT_CHUNKS + 1) * CHUNK
REL_CLIP = 128
N_REL = CHUNK + REL_CLIP
RMS_EPS = 1e-6
GN_EPS = 64e-5
D_SHIFT = 4 * D_RWKV + LORA_W + LORA_A
D_IN = D_SHIFT + 4 * D_ATT

kernel_name = 'hybrid_rwkv7_chunkattn_adaln_trunk'


def rms_norm(x, g):
    x32 = x.astype(jnp.float32)
    y = x32 * lax.rsqrt(jnp.mean(x32 * x32, axis=-1, keepdims=True) + RMS_EPS)
    return y.astype(x.dtype) * g


def token_shift(p, mu):
    prev = jnp.pad(p, ((0, 0), (1, 0), (0, 0)))[:, :-1]
    return p + mu * (prev - p)


def rwkv7_step(state, inp):
    r_t, w_t, k_t, v_t, a_t, b_t = inp
    sa = jnp.einsum('bhij,bhj->bhi', state, a_t)
    state = state * w_t[:, :, None, :] + sa[..., None] * b_t[:, :, None, :] + v_t[..., None] * k_t[:, :, None, :]
    y = jnp.einsum('bhij,bhj->bhi', state, r_t)
    return state, y


def rwkv7_mix(ps, w0, w2, a0, a2, k_k, k_a, r_k, lnx_g, lnx_b):
    b_, s_, _ = ps.shape
    f32 = jnp.float32
    r, k, v, g, wd, ad = jnp.split(ps, [D_RWKV, 2 * D_RWKV, 3 * D_RWKV, 4 * D_RWKV, 4 * D_RWKV + LORA_W], axis=-1)
    w_log = -jax.nn.softplus(-(w0 + jnp.tanh(wd) @ w2).astype(f32)) - 0.5
    decay = jnp.exp(-jnp.exp(w_log))
    a = jax.nn.sigmoid((a0 + ad @ a2).astype(f32))
    heads = lambda t: t.astype(f32).reshape(b_, s_, N_RWKV_HEADS, HEAD_DIM)
    r, k, v, decay, a = heads(r), heads(k), heads(v), heads(decay), heads(a)
    kk = k * k_k.astype(f32).reshape(N_RWKV_HEADS, HEAD_DIM)
    kk = kk / jnp.maximum(jnp.sqrt(jnp.sum(kk * kk, axis=-1, keepdims=True)), 1e-12)
    k = k * (1.0 + (a - 1.0) * k_a.astype(f32).reshape(N_RWKV_HEADS, HEAD_DIM))
    seq_first = lambda t: jnp.moveaxis(t, 1, 0)
    state0 = jnp.zeros((b_, N_RWKV_HEADS, HEAD_DIM, HEAD_DIM), f32)
    _, y = lax.scan(rwkv7_step, state0, (seq_first(r), seq_first(decay), seq_first(k), seq_first(v), seq_first(-kk), seq_first(kk * a)))
    y = jnp.moveaxis(y, 0, 1)
    mean = jnp.mean(y, axis=-1, keepdims=True)
    yc = y - mean
    y = yc * lax.rsqrt(jnp.mean(yc * yc, axis=-1, keepdims=True) + GN_EPS)
    y = y.reshape(b_, s_, D_RWKV) * lnx_g.astype(f32) + lnx_b.astype(f32)
    bonus = jnp.sum(r * k * r_k.astype(f32), axis=-1, keepdims=True) * v
    y = y + bonus.reshape(b_, s_, D_RWKV)
    return y.astype(ps.dtype), g


def chunk_band_attention(q, k, v, q_g, k_g, rel_bias):
    b_, s_, _ = q.shape
    nc = s_ // CHUNK
    pad = N_LEFT_CHUNKS * CHUNK
    def heads(t):
        return t.reshape(b_, s_, N_ATT_HEADS, HEAD_DIM).transpose(0, 2, 1, 3)
    q = heads(rms_norm(q.reshape(b_, s_, N_ATT_HEADS, HEAD_DIM), q_g).reshape(b_, s_, D_ATT))
    k = heads(rms_norm(k.reshape(b_, s_, N_ATT_HEADS, HEAD_DIM), k_g).reshape(b_, s_, D_ATT))
    v = heads(v)
    qc = q.reshape(b_, N_ATT_HEADS, nc, CHUNK, HEAD_DIM)
    def band(t):
        tp = jnp.pad(t, ((0, 0), (0, 0), (pad, 0), (0, 0))).reshape(b_, N_ATT_HEADS, nc + N_LEFT_CHUNKS, CHUNK, HEAD_DIM)
        return jnp.concatenate([tp[:, :, i:i + nc] for i in range(N_LEFT_CHUNKS + 1)], axis=3)
    kb, vb = band(k), band(v)
    s = jnp.einsum('bhnqd,bhnkd->bhnqk', qc, kb).astype(jnp.float32) * (HEAD_DIM ** -0.5)
    dist = jnp.arange(CHUNK)[:, None] + pad - jnp.arange(BAND)[None, :]
    rel_idx = jnp.clip(dist, -(CHUNK - 1), REL_CLIP) + (CHUNK - 1)
    bias = rel_bias[:, rel_idx].astype(jnp.float32)
    valid = (jnp.arange(nc)[:, None] * CHUNK + jnp.arange(BAND)[None, :] - pad) >= 0
    s = s + bias[None, :, None]
    s = jnp.where(valid[None, None, :, None, :], s, -1e30)
    p = jax.nn.softmax(s, axis=-1).astype(v.dtype)
    o = jnp.einsum('bhnqk,bhnkd->bhnqd', p, vb)
    return o.reshape(b_, N_ATT_HEADS, s_, HEAD_DIM).transpose(0, 2, 1, 3).reshape(b_, s_, D_ATT)


def setup_inputs(seed: int = 0) -> dict:
    key = jax.random.key(seed)
    ks = jax.random.split(key, 22)
    n = lambda i, shape: jax.random.normal(ks[i], shape, jnp.float32)
    L = DEPTH
    return {
        'x': n(0, (BATCH, SEQ, D_MODEL)),
        'c': n(1, (BATCH, D_MODEL)),
        'norm_g': 1.0 + 0.1 * n(2, (L, D_MODEL)),
        'w_ada': n(3, (L, D_MODEL, 3 * D_MODEL)) * (0.5 * D_MODEL ** -0.5),
        'b_ada': 0.02 * n(4, (L, 3 * D_MODEL)),
        'w_in': n(5, (L, D_MODEL, D_IN)) * (D_MODEL ** -0.5),
        'mu_shift': jax.random.uniform(ks[6], (L, D_SHIFT), jnp.float32),
        'w0': -1.0 + 0.5 * n(7, (L, D_RWKV)),
        'w2': n(8, (L, LORA_W, D_RWKV)) * (0.5 * LORA_W ** -0.5),
        'a0': 0.5 * n(9, (L, D_RWKV)),
        'a2': n(10, (L, LORA_A, D_RWKV)) * (0.5 * LORA_A ** -0.5),
        'k_k': 0.85 + 0.05 * n(11, (L, D_RWKV)),
        'k_a': 1.0 + 0.05 * n(12, (L, D_RWKV)),
        'r_k': 0.1 * n(13, (L, N_RWKV_HEADS, HEAD_DIM)),
        'lnx_g': 1.0 + 0.1 * n(14, (L, D_RWKV)),
        'lnx_b': 0.02 * n(15, (L, D_RWKV)),
        'q_norm_g': 1.0 + 0.1 * n(16, (L, HEAD_DIM)),
        'k_norm_g': 1.0 + 0.1 * n(17, (L, HEAD_DIM)),
        'rel_bias': 0.5 * n(18, (L, N_ATT_HEADS, N_REL)),
        'w_out': n(19, (L, D_MIX, D_MODEL)) * (D_MIX ** -0.5),
    }


def reference(x, c, norm_g, w_ada, b_ada, w_in, mu_shift, w0, w2, a0, a2, k_k, k_a, r_k, lnx_g, lnx_b, q_norm_g, k_norm_g, rel_bias, w_out):
    c_act = jax.nn.silu(c)
    for l in range(DEPTH):
        mod = c_act @ w_ada[l] + b_ada[l]
        shift, scale, gate = jnp.split(mod, 3, axis=-1)
        h = rms_norm(x, norm_g[l]) * (1.0 + scale[:, None, :]) + shift[:, None, :]
        proj = h @ w_in[l]
        ps = token_shift(proj[..., :D_SHIFT], mu_shift[l])
        pa = proj[..., D_SHIFT:]
        y_r, g_r = rwkv7_mix(ps, w0[l], w2[l], a0[l], a2[l], k_k[l], k_a[l], r_k[l], lnx_g[l], lnx_b[l])
        q, ka, va, g_a = jnp.split(pa, 4, axis=-1)
        y_a = chunk_band_attention(q, ka, va, q_norm_g[l], k_norm_g[l], rel_bias[l])
        y = jnp.concatenate([y_r * jax.nn.silu(g_r), y_a * jax.nn.silu(g_a)], axis=-1)
        x = x + gate[:, None, :] * (y @ w_out[l])
    return x
```

```python
import os as _os
import numpy as np
from contextlib import ExitStack
import concourse.bass as bass
import concourse.mybir as mybir
from concourse.bass_utils import run_bass_kernel_spmd

F32 = mybir.dt.float32
F32R = mybir.dt.float32r
BF16 = mybir.dt.bfloat16
AF = mybir.ActivationFunctionType
ALU = mybir.AluOpType

D = 1024
SEQ = 2048
NLAYER = 4
TB = 512
CH = 128
NC = TB // CH
NBLK = SEQ // TB
C0 = float(np.exp(-0.5))
RMS_EPS = 1e-6
GN_EPS = 64e-5
EPOCH = 8000

PV_NG, PV_MU, PV_W0, PV_A0, PV_KK, PV_KA, PV_RK, PV_LNG, PV_LNB, PV_QG, PV_KG, PV_BADA, PV_CB = (
    0, 8, 25, 29, 33, 37, 41, 45, 49, 53, 54, 55, 79)
NPV = 87


class Res:
    __slots__ = ("name", "lw", "rd", "excl")

    def __init__(self, name, excl=False):
        self.name = name
        self.lw = None
        self.rd = {}
        self.excl = excl


class V:
    __slots__ = ("ap", "res")

    def __init__(self, ap, res):
        self.ap = ap
        self.res = res

    def __getitem__(self, k):
        return V(self.ap[k], self.res)

    def re(self, pat, **kw):
        return V(self.ap.rearrange(pat, **kw), self.res)

    def bc(self, shape):
        return V(self.ap.broadcast_to(shape), self.res)

    def us(self, ax):
        return V(self.ap.unsqueeze(ax), self.res)


class Prog:
    ENGS = ("pe", "dve", "act", "pool", "sp")

    def __init__(self, nc, stack):
        self.nc = nc
        self.stack = stack
        self.ops = {e: [] for e in self.ENGS}
        self.cnt = {}
        self.sems = {}
        self.waited = {e: {} for e in self.ENGS}
        self.pend = {e: ([], []) for e in self.ENGS}

    def _sem_for(self, key, c):
        lst = self.sems.setdefault(key, [])
        idx = (c - 1) // EPOCH
        while len(lst) <= idx:
            lst.append(self.stack.enter_context(self.nc.semaphore("s_%s_%d" % (key.replace(":", "_"), len(lst)))))
        return lst[idx], (c - 1) % EPOCH + 1

    def _add_wait(self, eng, key, c):
        w = self.waited[eng]
        if w.get(key, 0) >= c:
            return
        w[key] = c
        mult = 16 if key.startswith("dma:") else 1
        sem, val = self._sem_for(key, c)
        self.ops[eng].append(lambda e, sem=sem, val=val * mult: e.wait_ge(sem, val))

    def _deps(self, eng, reads, writes):
        deps = {}

        def add(k, c, raw):
            if k == eng and eng == "pe":
                return
            if deps.get(k, 0) < c:
                deps[k] = c
        for r in reads:
            if r.lw is not None:
                add(r.lw[0], r.lw[1], True)
            if r.excl:
                for k, c in r.rd.items():
                    if k != eng:
                        add(k, c, False)
        for w in writes:
            if w.lw is not None:
                add(w.lw[0], w.lw[1], False)
            for k, c in w.rd.items():
                add(k, c, False)
        for k, c in deps.items():
            self._add_wait(eng, k, c)

    def op(self, eng, fn, reads=(), writes=(), signal=True):
        self._deps(eng, reads, writes)
        pr, pw = self.pend[eng]
        pr.extend(reads)
        pw.extend(writes)
        if not signal:
            self.ops[eng].append(lambda e, fn=fn: fn(e))
            return
        c = self.cnt.get(eng, 0) + 1
        self.cnt[eng] = c
        sem, _ = self._sem_for(eng, c)
        self.ops[eng].append(lambda e, fn=fn, sem=sem: fn(e).then_inc(sem, 1))
        for r in pr:
            r.rd[eng] = c
        for w in pw:
            w.lw = (eng, c)
            w.rd = {}
        self.pend[eng] = ([], [])

    def dma(self, queue, slot, fn, reads=(), writes=()):
        key = "dma:" + slot
        self._deps(queue, reads, writes)
        c = self.cnt.get(key, 0) + 1
        self.cnt[key] = c
        sem, _ = self._sem_for(key, c)
        self.ops[queue].append(lambda e, fn=fn, sem=sem: fn(e).then_inc(sem, 16))
        for r in reads:
            r.rd[key] = c
        for w in writes:
            w.lw = (key, c)
            w.rd = {}

    def wait_res(self, eng, res):
        if res.lw is not None:
            self._add_wait(eng, res.lw[0], res.lw[1])

    def run(self):
        with self.nc.Block() as block:
            @block.tensor
            def _(e):
                for f in self.ops["pe"]:
                    f(e)

            @block.vector
            def _(e):
                for f in self.ops["dve"]:
                    f(e)

            @block.scalar
            def _(e):
                for f in self.ops["act"]:
                    f(e)

            @block.gpsimd
            def _(e):
                for f in self.ops["pool"]:
                    f(e)

            @block.sync
            def _(e):
                for f in self.ops["sp"]:
                    f(e)


def build_program(nlayers, dbg=False, nblk=NBLK, stage=99, PIPE=True, W_G=1, W_P=2):
    nc = bass.Bass("TRN2", target_bir_lowering=False)
    L = nlayers
    d_x = nc.dram_tensor("xT", [D, SEQ], F32, kind="ExternalInput").ap()
    d_c = nc.dram_tensor("cfm", [128, 8], F32, kind="ExternalInput").ap()
    d_pv = nc.dram_tensor("pv", [L, 128, NPV], F32, kind="ExternalInput").ap()
    d_w2a2 = nc.dram_tensor("w2a2", [L, 128, 512], F32, kind="ExternalInput").ap()
    d_bias = nc.dram_tensor("biasT", [L, 128, 8 * 256], F32, kind="ExternalInput").ap()
    d_wada = nc.dram_tensor("w_ada", [L, D, 3 * D], F32, kind="ExternalInput").ap()
    d_win = nc.dram_tensor("w_in", [L, D, 4224], F32, kind="ExternalInput").ap()
    d_wout = nc.dram_tensor("w_out", [L, D, D], F32, kind="ExternalInput").ap()
    d_o = nc.dram_tensor("oT", [D, SEQ], F32, kind="ExternalOutput").ap()
    d_xv = d_x.rearrange("(k p) t -> p k t", p=128)
    d_ov = d_o.rearrange("(k p) t -> p k t", p=128)
    d_dbg = {}

    with ExitStack() as st:
        P = Prog(nc, st)

        def sb(name, shape, dt):
            t = st.enter_context(nc.sbuf_tensor(name, list(shape), dt))
            return V(t[tuple(slice(None) for _ in shape)], Res(name))

        def psum(name, shape, dt):
            t = st.enter_context(nc.psum_tensor(name, list(shape), dt))
            return V(t[tuple(slice(None) for _ in shape)], Res(name, excl=True))

        def rs(*vs):
            out = []
            for v in vs:
                if isinstance(v, V):
                    if isinstance(v.res, (list, tuple)):
                        out.extend(v.res)
                    else:
                        out.append(v.res)
            return out

        def split(v, n):
            rl = [Res("%s_%d" % (v.res.name, i)) for i in range(n)]
            v.res = rl
            return [V(v.ap[:, i], rl[i]) for i in range(n)]

        def apof(v):
            return v.ap if isinstance(v, V) else v

        def mmr(out, lhsT, rhs):
            P.op("pe", lambda e: e.matmul(out.ap, lhsT=lhsT.ap.bitcast(F32R), rhs=rhs.ap.bitcast(F32R)),
                 reads=rs(lhsT, rhs), writes=rs(out))

        def mm(out, lhsT, rhs, start=True, stop=True, sig=True, tp=None):
            kw = {}
            if tp is not None:
                kw["tile_position"] = tp
            P.op("pe", lambda e: e.matmul(out.ap, lhsT=lhsT.ap, rhs=rhs.ap, start=start, stop=stop, **kw),
                 reads=rs(lhsT, rhs), writes=rs(out), signal=sig)

        def tr(out, in_, idn, sig=True):
            P.op("pe", lambda e: e.transpose(out.ap, in_.ap, idn.ap), reads=rs(in_, idn), writes=rs(out), signal=sig)

        def act(out, in_, func, bias=None, scale=None):
            kw = {}
            if bias is not None:
                kw["bias"] = apof(bias)
            if scale is not None:
                kw["scale"] = apof(scale)
            P.op("act", lambda e: e.activation(out=out.ap, in_=in_.ap, func=func, **kw),
                 reads=rs(in_, bias, scale), writes=rs(out))

        def tt(eng, out, in0, in1, op):
            P.op(eng, lambda e: e.tensor_tensor(out=out.ap, in0=in0.ap, in1=in1.ap, op=op),
                 reads=rs(in0, in1), writes=rs(out))

        def ts(eng, out, in0, s1, op0, s2=None, op1=None):
            if op1 is None:
                P.op(eng, lambda e: e.tensor_scalar(out=out.ap, in0=in0.ap, scalar1=apof(s1), scalar2=None, op0=op0),
                     reads=rs(in0, s1), writes=rs(out))
            else:
                P.op(eng, lambda e: e.tensor_scalar(out=out.ap, in0=in0.ap, scalar1=apof(s1), scalar2=apof(s2),
                                                    op0=op0, op1=op1),
                     reads=rs(in0, s1, s2), writes=rs(out))

        def stt(out, in0, scalar, in1, op0, op1):
            P.op("dve", lambda e: e.scalar_tensor_tensor(out=out.ap, in0=in0.ap, scalar=apof(scalar), in1=in1.ap,
                                                         op0=op0, op1=op1),
                 reads=rs(in0, scalar, in1), writes=rs(out))

        def cp(eng, out, in_):
            if eng == "act":
                P.op("act", lambda e: e.copy(out=out.ap, in_=in_.ap), reads=rs(in_), writes=rs(out))
            else:
                P.op(eng, lambda e: e.tensor_copy(out=out.ap, in_=in_.ap), reads=rs(in_), writes=rs(out))

        def mset(eng, out, val):
            P.op(eng, lambda e: e.memset(out.ap, val), writes=rs(out))

        def asel(out, pattern, cmp, fill, base, cm):
            P.op("pool", lambda e: e.affine_select(out=out.ap, in_=out.ap, pattern=pattern, compare_op=cmp, fill=fill,
                                                   base=base, channel_multiplier=cm), reads=rs(out), writes=rs(out))

        def dma(queue, slot, out, in_, reads=(), writes=()):
            P.dma(queue, slot, lambda e: e.dma_start(out=apof(out), in_=apof(in_)),
                  reads=list(reads) + rs(in_), writes=list(writes) + rs(out))

        pa = psum("pa", [128, 512], F32)
        pb = psum("pb", [128, 512], F32)
        pm = psum("pm", [128, 512], F32)
        ptr = psum("ptr", [128, 1024], BF16)
        pw = [psum("pw%d" % i, [128, 512], F32) for i in range(4)]

        ident = sb("ident", [128, 128], BF16)
        onesbf = sb("onesbf", [128, 128], BF16)
        onesbd = sb("onesbd", [128, 128], F32)
        onesbdb = sb("onesbdb", [128, 128], BF16)
        sqb = [sb("sqb%d" % i, [128, TB], BF16) for i in range(3)]
        mskscan = sb("mskscan", [128, 512], F32)
        mk4 = sb("mk4", [CH, 4 * CH], F32)
        mksl = sb("mksl", [CH, 2 * CH], F32)
        mset("pool", ident, 0.0)
        asel(ident, [[-1, 128]], ALU.not_equal, 1.0, 0, 1)
        mset("pool", onesbf, 1.0)
        mset("pool", onesbd, 0.0)
        mset("pool", onesbd[0:64, 0:64], 1.0)
        mset("pool", onesbd[64:128, 64:128], 1.0)
        cp("pool", onesbdb, onesbd)
        mset("pool", mskscan, 1.0)
        mset("pool", mskscan.re("p (c t) -> p c t", t=CH)[:, :, 0:1], 0.0)
        mset("pool", mk4, 1.0)
        for i in range(4):
            asel(mk4[:, i * CH:(i + 1) * CH], [[1, CH]], ALU.is_ge, 0.0, -1 if i % 2 == 0 else 0, -1)
        mset("pool", mksl, 1.0)
        for i in range(2):
            asel(mksl[:, i * CH:(i + 1) * CH], [[-1, CH]], ALU.is_ge, 0.0, -1, 1)

        cfm = sb("cfm_s", [128, 8], F32)
        cact = sb("cact", [128, 8], F32)
        cactb = sb("cactb", [128, 8], BF16)
        pvs = sb("pvs", [128, NPV], F32)
        omm = sb("omm", [128, 17], F32)
        mod = sb("mod", [128, 24], F32)
        g1 = sb("g1", [128, 8], F32)
        w2a2b = sb("w2a2b", [128, 512], BF16)
        biasT = sb("biasT_s", [128, 8, 256], F32)
        carry = sb("carry", [128, 17], F32)

        hT = sb("hT", [128, 8, TB], BF16)
        TA = [sb("TA%d" % i, [128, TB], F32) for i in range(8)]
        sq = [sb("sq%d" % i, [128, TB], BF16) for i in range(2)]
        NW = 4
        wst = [sb("wst%d" % i, [128, 8, 128], F32) for i in range(NW)]
        wbf = [sb("wbf%d" % i, [128, 8, 128], BF16) for i in range(NW)]
        Pb = [sb("Pb%d" % i, [128, TB + 4], F32) for i in range(2)]
        la = sb("la", [128, TB], BF16)
        rkvg = [sb("rkvg%d" % i, [128, TB], F32) for i in range(4)]
        tb16 = [sb("tb16_%d" % i, [128, TB], BF16) for i in range(4)]

        At = sb("At", [128, 2, TB], BF16)
        ARbd = sb("ARbd", [128, 2, NC, 4 * CH], BF16)
        Bbd = sb("Bbd", [128, 2, NC, 2 * CH], BF16)
        Bt = sb("Bt", [128, 2, TB], BF16)
        Kt = sb("Kt", [128, 2, TB], BF16)
        Bh = sb("Bh", [CH, 2, NC, 128], BF16)
        Kh = sb("Kh", [CH, 2, NC, 128], BF16)
        Vt = sb("Vt", [CH, 2, NC, 128], BF16)
        Gab = sb("Gab", [CH, 2, NC, 4 * CH], BF16)
        Gk = sb("Gk", [CH, 2, NC, 4 * CH], BF16)
        Aab = sb("Aab", [CH, NC, 2, CH], BF16)
        Pp = [sb("Pp%d" % i, [CH, NC, 2, CH], BF16) for i in range(2)]
        Qq = [sb("Qq%d" % i, [CH, NC, 2, CH], BF16) for i in range(2)]
        TT = sb("TT", [CH, 2, NC, 2, CH], BF16)
        gC = sb("gC", [128, 2, NC], F32)
        gR = sb("gR", [128, 2, NC], F32)
        H32 = sb("H32", [128, 4, 128], F32)
        Hbf = sb("Hbf", [128, 4, 128], BF16)
        tmpH = sb("tmpH", [128, 2, 128], F32)
        Xs = sb("Xs", [CH, 2, 128], BF16)
        Us = sb("Us", [CH, 2, 128], BF16)
        Yfm = sb("Yfm", [128, 2, TB], F32)
        BV = sb("BV", [128, 2, TB], F32)
        SG = sb("SG", [128, 2, TB], BF16)

        kTw = [sb("kTw%d" % i, [128, 4, TB], BF16) for i in range(2)]
        Vw = [sb("Vw%d" % i, [128, 4, 8, 65], BF16) for i in range(2)]
        qbd = sb("qbd", [128, 2, TB], BF16)
        sga = sb("sga", [128, TB], BF16)
        vfm = sb("vfm", [128, TB], BF16)
        PTc = sb("PTc", [128, 2, 5, 128], BF16)
        PT = [PTc[:, i] for i in range(2)]
        sb34c = sb("sb34c", [128, 2, 2, 128], F32)
        sb34 = [sb34c[:, i] for i in range(2)]
        rcpc = sb("rcpc", [128, 2], F32)
        osbc = sb("osbc", [128, 2, 65], F32)
        Otm = [sb("Otm%d" % i, [128, 128], BF16) for i in range(2)]
        ymix = sb("ymix", [128, 8, TB], BF16)
        xo = [sb("xo%d" % i, [128, TB], F32) for i in range(2)]
        xd = [Res("xd%d" % b) for b in range(NBLK)]
        GT = [sb("GT%d" % i, [128, TB], F32) for i in range(2)]
        rstd = GT[0]
        laf = rkvg[0]
        At_, ARbd_, Bbd_, Bt_, Kt_ = split(At, 2), split(ARbd, 2), split(Bbd, 2), split(Bt, 2), split(Kt, 2)
        Bh_, Kh_, Vt_ = split(Bh, 2), split(Kh, 2), split(Vt, 2)
        Gab_, Gk_, TT_ = split(Gab, 2), split(Gk, 2), split(TT, 2)
        gC_, gR_ = split(gC, 2), split(gR, 2)
        Yfm_, BV_, SG_ = split(Yfm, 2), split(BV, 2), split(SG, 2)
        H32_, Hbf_ = split(H32, 4), split(Hbf, 4)
        ymix_ = split(ymix, 8)

        mset("pool", ARbd, 0.0)
        mset("pool", Bbd, 0.0)
        mset("pool", qbd, 0.0)
        for i in range(2):
            mset("pool", Vw[i], 1.0)

        wcount = [0]

        class Ring:
            def __init__(self, slots, ahead):
                self.slots, self.ahead, self.i = slots, ahead, 0

            def next(self):
                s_ = self.slots[self.i % len(self.slots)]
                self.i += 1
                return s_
        ring_all = Ring((0, 1, 2, 3), 2)
        ring_a = Ring((0, 1), 1)
        ring_b = Ring((2, 3), 1)

        def load_w(dram_ap, cast=True, ring=None):
            i = (ring or ring_all).next()
            wcount[0] += 1
            if _os.environ.get("KNODMA") and wcount[0] > 200:
                return wbf[i] if cast else wst[i]
            dma("sp", "wst%d" % i, wst[i], dram_ap.rearrange("(k p) n -> p k n", p=128))
            if cast:
                cp("act" if wcount[0] % 2 == 0 else "dve", wbf[i], wst[i])
                return wbf[i]
            return wst[i]

        dma("sp", "cfm", cfm, d_c)
        act(cact, cfm, AF.Silu)
        cp("dve", cactb, cact)

        class _Stop(Exception):
            pass

        import os as _os
        _tgt = _os.environ.get("KSTOP", "")

        def chk(n):
            if stage < n:
                raise _Stop()

        def chk2(*tag):
            if _tgt and _tgt == ",".join(str(t) for t in tag):
                raise _Stop()

        try:
          for l in range(L):
              dma("sp", "pv", pvs, d_pv[l])
              dma("sp", "w2a2", TA[0], d_w2a2[l])
              dma("sp", "biasT", biasT.re("p h j -> p (h j)"), d_bias[l])
              cp("pool", w2a2b, TA[0])
              for h in range(8):
                  mset("pool", biasT[64:128, h, 128:192], -30000.0)
              ts("dve", omm, pvs[:, PV_MU:PV_MU + 17], -1.0, ALU.mult, 1.0, ALU.add)
              mset("pool", carry, 0.0)
              mset("pool", H32, 0.0)
              mset("pool", Hbf, 0.0)
              wq_ = [load_w(d_wada[l][:, ct_ * 128:(ct_ + 1) * 128]) for ct_ in range(2)]
              for ctile in range(24):
                  if ctile + 2 < 24:
                      wq_.append(load_w(d_wada[l][:, (ctile + 2) * 128:(ctile + 3) * 128]))
                  w = wq_[ctile]
                  for k in range(8):
                      mm(pm[:, ctile:ctile + 1], w[:, k, :], cactb[:, k:k + 1], start=(k == 0), stop=(k == 7),
                         sig=(k == 7))
              tt("dve", mod, pm[:, 0:24], pvs[:, PV_BADA:PV_BADA + 24], ALU.add)
              stt(g1, mod[:, 8:16], 1.0, pvs[:, PV_NG:PV_NG + 8], ALU.add, ALU.mult)
              shift = mod[:, 0:8]
              gate = mod[:, 16:24]
              chk(1)

              pcount = [0]

              def make_block(b):
                  t0 = b * TB
                  slot = b % 2
                  src = d_xv if l == 0 else d_ov
                  xbv = [TA[k] for k in range(8)]

                  def colproj(j, w=None):
                      if w is None:
                          w = load_w(d_win[l][:, j * 128:(j + 1) * 128])
                      ps = pa if pcount[0] % 2 == 0 else pb
                      pcount[0] += 1
                      for k in range(8):
                          mm(ps, w[:, k, :], hT[:, k, :], start=(k == 0), stop=(k == 7), sig=(k == 7))
                      return ps

                  def shifted(j, out, w=None):
                      ps = colproj(j, w=w)
                      pbuf = Pb[j % 2]
                      cp("act", pbuf[:, 0:1], carry[:, j:j + 1])
                      act(pbuf[:, 1:TB + 1], ps, AF.Copy, scale=pvs[:, PV_MU + j:PV_MU + j + 1])
                      cp("act", carry[:, j:j + 1], pbuf[:, TB:TB + 1])
                      stt(out, ps, omm[:, j:j + 1], pbuf[:, 0:TB], ALU.mult, ALU.add)

                  def front(ring):
                      for k in range(8):
                          dma("act", "xb%d" % k, TA[k], src[:, k, t0:t0 + TB], reads=[xd[b]])
                      for k in range(8):
                          act(sq[k % 2], xbv[k], AF.Square)
                          mm(pm, onesbf, sq[k % 2], start=(k == 0), stop=(k == 7), sig=True)
                          if k % 2 == 1:
                              yield
                      act(rstd, pm, AF.Ln, bias=RMS_EPS, scale=1.0 / D)
                      act(rstd, rstd, AF.Exp, scale=-0.5)
                      yield
                      for k in range(8):
                          tt("dve", xbv[k], xbv[k], rstd, ALU.mult)
                          act(hT[:, k, :], xbv[k], AF.Identity, bias=shift[:, k:k + 1], scale=g1[:, k:k + 1])
                          yield
                      shifted(16, laf, w=load_w(d_win[l][:, 16 * 128:17 * 128], ring=ring))
                      act(la[0:64, :], laf[0:64, :], AF.Tanh)
                      cp("act", la[64:128, :], laf[64:128, :])
                      yield

                  def prep(hp, ring=None):
                      ring = ring or ring_all
                      oe = "dve"
                      j = hp % 2
                      r_, k_, v_, g_ = rkvg
                      T = TA
                      col = lambda base: pvs[:, base + hp:base + hp + 1]
                      tiles = [hp, 4 + hp, 8 + hp, 12 + hp]
                      ah = ring.ahead
                      wq = [load_w(d_win[l][:, tiles[i_] * 128:(tiles[i_] + 1) * 128], ring=ring) for i_ in range(ah)]
                      for i, dst in enumerate((r_, k_, v_, g_)):
                          if i + ah < 4:
                              wq.append(load_w(d_win[l][:, tiles[i + ah] * 128:(tiles[i + ah] + 1) * 128], ring=ring))
                          shifted(tiles[i], dst, w=wq[i])
                          yield
                      DM = GT[1]
                      S3 = T[1].re("p (c t) -> p c t", t=CH)

                      def path_decay():
                          mm(pm, w2a2b[0:64, hp * 128:(hp + 1) * 128], la[0:64, :])
                          act(T[0], pm, AF.Sigmoid, bias=col(PV_W0))
                          yield
                          P.op("dve", lambda e: e.tensor_tensor_scan(out=T[1].ap, data0=mskscan.ap, data1=T[0].ap,
                                                                     initial=0.0, op0=ALU.mult, op1=ALU.add),
                               reads=rs(mskscan, T[0]), writes=rs(T[1]))
                          yield
                          tt("dve", DM.re("p (c t) -> p c t", t=CH), S3, S3[:, :, CH // 2 - 1:CH // 2].bc([128, NC, CH]),
                             ALU.subtract)
                          yield
                          tt("dve", T[6], DM, T[0], ALU.subtract)
                          yield
                          tt(oe, T[7].re("p (c t) -> p c t", t=CH), S3[:, :, CH - 1:CH].bc([128, NC, CH]), S3,
                             ALU.subtract)
                          yield
                          act(gC_[j], S3[:, :, CH - 1], AF.Exp, scale=-C0)
                          act(gR_[j], S3[:, :, CH // 2 - 1], AF.Exp, scale=-C0)
                          yield
                          act(T[0], DM, AF.Exp, scale=-C0)
                          yield
                          act(T[1], DM, AF.Exp, scale=C0)
                          yield
                          act(T[6], T[6], AF.Exp, scale=-C0)
                          yield
                          act(T[7], T[7], AF.Exp, scale=-C0)
                          yield

                      def path_kk():
                          mm(pm, w2a2b[64:128, hp * 128:(hp + 1) * 128], la[64:128, :], tp=(64, 0))
                          act(T[2], pm, AF.Sigmoid, bias=col(PV_A0))
                          yield
                          act(sqb[0], k_, AF.Square, scale=col(PV_KK))
                          yield
                          mm(pm, onesbdb, sqb[0])
                          act(T[3], pm, AF.Ln, bias=1e-18)
                          yield
                          act(T[3], T[3], AF.Exp, scale=-0.5)
                          yield
                          stt(T[4], k_, col(PV_KK), T[3], ALU.mult, ALU.mult)
                          yield
                          ts("dve", T[3], T[2], -1.0, ALU.add, col(PV_KA), ALU.mult)
                          yield
                          stt(T[3], T[3], 1.0, k_, ALU.add, ALU.mult)
                          yield
                          tt("dve", T[5], T[4], T[2], ALU.mult)
                          yield
                          stt(sqb[0], r_, col(PV_RK), T[3], ALU.mult, ALU.mult)
                          yield
                          mm(pm, onesbdb, sqb[0])
                          tt("dve", BV_[j], pm, v_, ALU.mult)
                          yield
                      ga, gb = path_decay(), path_kk()
                      live = [ga, gb]
                      while live:
                          for g_i in list(live):
                              try:
                                  next(g_i)
                              except StopIteration:
                                  live.remove(g_i)
                          yield
                      stt(At_[j], T[4], -1.0, T[6], ALU.mult, ALU.mult)
                      tt("dve", Bt_[j], T[5], T[1], ALU.mult)
                      yield
                      tt(oe, Kt_[j], T[3], T[1], ALU.mult)
                      A3 = At_[j].re("p (c t) -> p c t", t=CH)
                      B3 = Bt_[j].re("p (c t) -> p c t", t=CH)
                      tt(oe, tb16[0], r_, T[0], ALU.mult)
                      R3 = tb16[0].re("p (c t) -> p c t", t=CH)
                      for h in range(2):
                          pr = slice(h * 64, (h + 1) * 64)
                          cp("dve", ARbd_[j][pr, :, h * 2 * CH:h * 2 * CH + CH], A3[pr])
                          cp("act", ARbd_[j][pr, :, h * 2 * CH + CH:(h + 1) * 2 * CH], R3[pr])
                          cp("act", Bbd_[j][pr, :, h * CH:(h + 1) * CH], B3[pr])
                          yield
                      tt(oe, tb16[3], T[5], T[7], ALU.mult)
                      tt(oe, tb16[1], T[3], T[7], ALU.mult)
                      cp("act", tb16[2], v_)
                      yield
                      for n_, (src_, dst_) in enumerate(((tb16[3], Bh_[j]), (tb16[1], Kh_[j]), (tb16[2], Vt_[j]))):
                          for c in range(NC):
                              tr(ptr[:, c * 128:(c + 1) * 128], src_[:, c * CH:(c + 1) * CH], ident, sig=(c == NC - 1))
                          yield
                          cp("act" if n_ < 2 else "dve", dst_.re("p c n -> p (c n)"), ptr[:, 0:NC * 128])
                      act(SG_[j], g_, AF.Silu)
                      yield

                  def gmat(hp, wide=False):
                      j = hp % 2
                      bP = (pw[2], pw[3])
                      bQ = (pw[0], pw[1]) if wide else bP
                      for c in range(NC):
                          mm(pw[2], Bt_[j][:, c * CH:(c + 1) * CH], ARbd_[j][:, c, :])
                          mm(pw[3], Kt_[j][:, c * CH:(c + 1) * CH], ARbd_[j][:, c, :])
                          yield
                          tt("dve", Gab_[j][:, c, :], pw[2], mk4, ALU.mult)
                          tt("dve", Gk_[j][:, c, :], pw[3], mk4, ALU.mult)
                      for half in range(2):
                          for c in (2 * half, 2 * half + 1):
                              mm(pw[2 + half][:, (c % 2) * 2 * CH:(c % 2 + 1) * 2 * CH], At_[j][:, c * CH:(c + 1) * CH],
                                 Bbd_[j][:, c, :], sig=(c % 2 == 1))
                      yield
                      for half in range(2):
                          tt("dve", Aab[:, 2 * half:2 * half + 2, :, :].re("p c h s -> p c (h s)"),
                             pw[2 + half].re("p (c n) -> p c n", n=2 * CH), mksl.us(1).bc([CH, 2, 2 * CH]), ALU.mult)
                      G5 = Gab_[j].re("p c (h k t) -> p c h k t", h=2, k=2)
                      Q0 = G5[:, :, :, 0, :]
                      TTj = TT_[j]
                      tt("dve", TTj, Q0, ident.us(1).us(1).bc([CH, NC, 2, CH]), ALU.add)
                      Pc, Qc = Aab, Q0
                      yield

                      def grp(half, lhs, rhs_, banks=None):
                          banks = banks or bP
                          for c in (2 * half, 2 * half + 1):
                              for h in range(2):
                                  o = ((c % 2) * 2 + h) * CH
                                  mm(banks[half][:, o:o + CH], lhs[:, c, h, :], rhs_[:, c, h, :],
                                     sig=(c % 2 == 1 and h == 1))
                      NLEV = 6
                      for lev in range(1, NLEV + 1):
                          Pn, Qn = Pp[lev % 2], Qq[lev % 2]
                          grp(0, Qc, Pc)
                          grp(1, Qc, Pc)
                          if wide and lev < NLEV:
                              grp(0, Pc, Qc, bQ)
                              grp(1, Pc, Qc, bQ)
                          yield
                          cp("dve", Pn[:, 0:2].re("p c h t -> p (c h t)"), pw[2])
                          cp("act", Pn[:, 2:4].re("p c h t -> p (c h t)"), pw[3])
                          if lev < NLEV:
                              if not wide:
                                  grp(0, Pc, Qc)
                                  grp(1, Pc, Qc)
                                  yield
                              cp("act", Qn[:, 0:2].re("p c h t -> p (c h t)"), bQ[0])
                              cp("dve", Qn[:, 2:4].re("p c h t -> p (c h t)"), bQ[1])
                          grp(0, Pn, TTj)
                          grp(1, Pn, TTj)
                          yield
                          for half in range(2):
                              tt("dve", TTj[:, 2 * half:2 * half + 2].re("p c h t -> p (c h t)"), pw[2 + half],
                                 TTj[:, 2 * half:2 * half + 2].re("p c h t -> p (c h t)"), ALU.add)
                          Pc, Qc = Pn, Qn
                      yield

                  def chain(pas):
                      hs = slice(pas * 2, pas * 2 + 2)
                      psX = pw[2][:, 0:256]
                      psU = pw[2][:, 256:512]
                      psY = pw[3]
                      psH = pw[2][:, 0:256]
                      for c in range(NC):
                          tt("dve", Hbf[:, hs, :], H32[:, hs, :], gR[:, :, c:c + 1].bc([128, 2, 128]), ALU.mult)
                          tt("pool", tmpH, H32[:, hs, :], gC[:, :, c:c + 1].bc([128, 2, 128]), ALU.mult)
                          yield
                          for j in range(2):
                              hp = pas * 2 + j
                              for h in range(2):
                                  mm(psX[:, j * 128 + h * 64:j * 128 + h * 64 + 64],
                                     Gk_[j][:, c, h * 2 * CH:h * 2 * CH + CH], Vt_[j][:, c, h * 64:(h + 1) * 64],
                                     start=(h == 0), stop=False, sig=False)
                              mm(psX[:, j * 128:(j + 1) * 128], At_[j][:, c * CH:(c + 1) * CH], Hbf_[hp],
                                 start=False, stop=True, sig=(j == 1))
                          yield
                          cp("dve", Xs.re("p j n -> p (j n)"), psX)
                          yield
                          for j in range(2):
                              for h in range(2):
                                  mm(psU[:, j * 128 + h * 64:j * 128 + h * 64 + 64], TT_[j][:, c, h, :],
                                     Xs[:, j, h * 64:(h + 1) * 64], sig=(j == 1 and h == 1))
                          yield
                          cp("act", Us.re("p j n -> p (j n)"), psU)
                          yield
                          for j in range(2):
                              hp = pas * 2 + j
                              Rbd = ARbd_[j][:, c, :].re("p (h k t) -> p h k t", h=2, k=2)[:, :, 1, :]
                              Gb5 = Gab_[j][:, c, :].re("p (h k t) -> p h k t", h=2, k=2)[:, :, 1, :]
                              Gk5 = Gk_[j][:, c, :].re("p (h k t) -> p h k t", h=2, k=2)[:, :, 1, :]
                              mm(psH[:, j * 128:(j + 1) * 128], Bh_[j][:, c, :], Us[:, j, :], start=True, stop=False,
                                 sig=False)
                              mm(psH[:, j * 128:(j + 1) * 128], Kh_[j][:, c, :], Vt_[j][:, c, :], start=False,
                                 stop=True, sig=False)
                              mm(psY[:, j * 2 * CH:(j + 1) * 2 * CH], Hbf_[hp], Rbd, start=True, stop=False, sig=False)
                              mm(psY[:, j * 2 * CH:(j + 1) * 2 * CH], Us[:, j, :], Gb5, start=False, stop=False, sig=False)
                              mm(psY[:, j * 2 * CH:(j + 1) * 2 * CH], Vt_[j][:, c, :], Gk5, start=False, stop=True,
                                 sig=(j == 1))
                          yield
                          for h in range(2):
                              pr = slice(h * 64, (h + 1) * 64)
                              tt("dve", H32[pr, hs, h * 64:(h + 1) * 64],
                                 psH[pr, :].re("p (j n) -> p j n", n=128)[:, :, h * 64:(h + 1) * 64],
                                 tmpH[pr, :, h * 64:(h + 1) * 64], ALU.add)
                          yield
                          for h in range(2):
                              pr = slice(h * 64, (h + 1) * 64)
                              cp("act", Yfm[pr, :, c * CH:(c + 1) * CH],
                                 psY[pr, :].re("p (j n) -> p j n", n=2 * CH)[:, :, h * CH:(h + 1) * CH])
                          yield
                      for j in range(2):
                          hp = pas * 2 + j
                          col = lambda base: pvs[:, base + hp:base + hp + 1]
                          mm(pw[3], onesbd, Yfm_[j])
                          stt(GT[0], pw[3], -1.0 / 64, Yfm_[j], ALU.mult, ALU.add)
                          yield
                          act(sqb[1], GT[0], AF.Square)
                          mm(pw[3], onesbdb, sqb[1])
                          yield
                          act(GT[1], pw[3], AF.Ln, bias=GN_EPS, scale=1.0 / 64)
                          act(GT[1], GT[1], AF.Exp, scale=-0.5)
                          yield
                          tt("dve", GT[0], GT[0], GT[1], ALU.mult)
                          act(GT[0], GT[0], AF.Identity, bias=col(PV_LNB), scale=col(PV_LNG))
                          yield
                          tt("dve", GT[0], GT[0], BV_[j], ALU.add)
                          tt("dve", ymix_[hp], GT[0], SG_[j], ALU.mult)
                          yield

                  def attn(hp):
                      T = TA
                      tiles = [17 + hp, 21 + hp, 25 + hp, 29 + hp]
                      wq = [load_w(d_win[l][:, tiles[i_] * 128:(tiles[i_] + 1) * 128]) for i_ in range(2)]

                      def proj(i):
                          if i + 2 < 4:
                              wq.append(load_w(d_win[l][:, tiles[i + 2] * 128:(tiles[i + 2] + 1) * 128]))
                          return colproj(tiles[i], w=wq[i])
                      for which in range(2):
                          ps = proj(which)
                          yield
                          act(sqb[2], ps, AF.Square)
                          mm(pm, onesbdb, sqb[2])
                          act(T[1], pm, AF.Ln, bias=RMS_EPS, scale=1.0 / 64)
                          act(T[1], T[1], AF.Exp, scale=-0.5)
                          yield
                          gcol = pvs[:, PV_QG + which:PV_QG + which + 1]
                          if which == 0:
                              for h in range(2):
                                  pr = slice(h * 64, (h + 1) * 64)
                                  stt(qbd[pr, h, :], ps[pr, :], gcol[pr, :], T[1][pr, :], ALU.mult, ALU.mult)
                          else:
                              stt(kTw[slot][:, hp, :], ps, gcol, T[1], ALU.mult, ALU.mult)
                          yield
                      ps = proj(2)
                      yield
                      cp("act", vfm, ps)
                      for qt in range(4):
                          tr(ptr[:, qt * 128:(qt + 1) * 128], vfm[:, qt * 128:(qt + 1) * 128], ident, sig=(qt == 3))
                      yield
                      cp("dve", Vw[slot][:, :, 2 * hp:2 * hp + 2, 0:64],
                         ptr[:, 0:512].re("p (q h d) -> p q h d", q=4, h=2))
                      ps = proj(3)
                      yield
                      act(sga, ps, AF.Silu)
                      sbank = ((pw[0], pw[1]), (pa, pb))
                      for qt in range(4):
                          kts = [kt for kt in range(5) if t0 + qt * 128 - 512 + kt * 128 >= 0]
                          locs = {}
                          for kt in kts:
                              tok = qt * 128 - 512 + kt * 128
                              locs[kt] = (slot, tok) if tok >= 0 else (1 - slot, TB + tok)
                          lo = [kt for kt in kts if kt < 3]
                          hi0 = 3 if 3 in kts else 4
                          for h in range(2):
                              b0, b1 = sbank[h]
                              for kt in kts:
                                  ksl, kof = locs[kt]
                                  dst = b0[:, kt * 128:(kt + 1) * 128] if kt < 3 else b1[:, (kt - 3) * 128:(kt - 2) * 128]
                                  mm(dst, kTw[ksl][:, hp, kof:kof + 128], qbd[:, h, qt * 128:(qt + 1) * 128],
                                     sig=(kt >= 3))
                          yield
                          for h in range(2):
                              b0, b1 = sbank[h]
                              head = 2 * hp + h
                              cb = pvs[:, PV_CB + head:PV_CB + head + 1]
                              if lo:
                                  act(PT[h][:, lo[0]:3, :], b0[:, lo[0] * 128:384].re("p (k q) -> p k q", q=128),
                                      AF.Exp, bias=cb, scale=0.125)
                              if 3 in kts:
                                  stt(sb34[h].re("p k q -> p (k q)"), b1[:, 0:256], 0.125,
                                      biasT[:, head, :], ALU.mult, ALU.add)
                              else:
                                  stt(sb34[h][:, 1, :], b1[:, 128:256], 0.125, biasT[:, head, 128:256], ALU.mult, ALU.add)
                          if 0 in lo:
                              mset("dve", PTc[0:64, :, 0, 64:128], 0.0)
                          yield
                          act(PTc[:, :, hi0:5, :], sb34c[:, :, hi0 - 3:2, :], AF.Exp)
                          yield
                          for h in range(2):
                              head = 2 * hp + h
                              psO = pm[:, h * 128:h * 128 + 65]
                              for n, kt in enumerate(kts):
                                  ksl, kof = locs[kt]
                                  mm(psO, PT[h][:, kt, :], Vw[ksl][:, kof // 128, head, :],
                                     start=(n == 0), stop=(n == len(kts) - 1), sig=(n == len(kts) - 1))
                          yield
                          cp("act", osbc, pm[:, 0:256].re("p (h n) -> p h n", n=128)[:, :, 0:65])
                          P.op("dve", lambda e: e.reciprocal(out=rcpc.ap, in_=osbc.ap[:, :, 64]),
                               reads=rs(osbc), writes=rs(rcpc))
                          yield
                          tt("dve", Otm[qt % 2].re("p (h d) -> p h d", h=2), osbc[:, :, 0:64],
                             rcpc.us(2).bc([128, 2, 64]), ALU.mult)
                          yield
                          tr(ptr[:, 512 + (qt % 2) * 128:512 + (qt % 2 + 1) * 128], Otm[qt % 2], ident)
                          yield
                          tt("dve", ymix_[4 + hp][:, qt * 128:(qt + 1) * 128],
                             ptr[:, 512 + (qt % 2) * 128:512 + (qt % 2 + 1) * 128], sga[:, qt * 128:(qt + 1) * 128],
                             ALU.mult)
                          yield

                  def seq(*gens):
                      for g in gens:
                          yield from g

                  def interleave(*gens):
                      gens = [g if isinstance(g, tuple) else (g, 1) for g in gens]
                      while gens:
                          for item in list(gens):
                              g, w_ = item
                              for _ in range(w_):
                                  try:
                                      next(g)
                                  except StopIteration:
                                      gens.remove(item)
                                      break

                  def mid():
                      for pas in range(2):
                          if pas > 0:
                              interleave(prep(2 * pas))
                          interleave((gmat(2 * pas), W_G), (prep(2 * pas + 1), W_P))
                          interleave(gmat(2 * pas + 1), attn(2 * pas))
                          interleave(chain(pas), attn(2 * pas + 1))

                  def outp(ring):
                      ah = ring.ahead
                      wq = [load_w(d_wout[l][:, jj * 128:(jj + 1) * 128], ring=ring) for jj in range(ah)]
                      for j in range(8):
                          if j + ah < 8:
                              wq.append(load_w(d_wout[l][:, (j + ah) * 128:(j + ah + 1) * 128], ring=ring))
                          w = wq[j]
                          ps = pa if pcount[0] % 2 == 0 else pb
                          pcount[0] += 1
                          dma("act", "xo_in%d" % (j % 2), xo[j % 2], src[:, j, t0:t0 + TB], reads=[xd[b]])
                          for k in range(8):
                              mm(ps, w[:, k, :], ymix_[k], start=(k == 0), stop=(k == 7), sig=(k == 7))
                          stt(xo[j % 2], ps, gate[:, j:j + 1], xo[j % 2], ALU.mult, ALU.add)
                          dma("act", "xo%d" % (j % 2), d_ov[:, j, t0:t0 + TB], xo[j % 2], writes=[xd[b]])
                          yield

                  return front, prep, mid, outp, interleave, seq

              blocks = [make_block(b) for b in range(nblk)]
              f0, p0, _, _, interleave, seq = blocks[0]
              interleave(seq(f0(ring_all), p0(0)))
              for b in range(nblk):
                  _, _, mid_b, out_b, _, _ = blocks[b]
                  mid_b()
                  if b + 1 < nblk:
                      f1, p1, _, _, _, _ = blocks[b + 1]
                      interleave(out_b(ring_b), seq(f1(ring_a), p1(0, ring_a)))
                  else:
                      interleave(out_b(ring_all))

        except _Stop:
            pass

        for name, (v, shape) in d_dbg.items():
            dt = nc.dram_tensor("dbg_" + name, shape, F32, kind="ExternalOutput").ap()
            P.dma("sp", "dbg_" + name, (lambda e, v=v, dt=dt: e.dma_start(out=dt, in_=v.ap)), reads=rs(v))
        for key in list(P.cnt.keys()):
            if key.startswith("dma:xo") or key.startswith("dma:dbg"):
                P._add_wait("sp", key, P.cnt[key])
        P.run()
    return nc


def _host_layout(inp, nl):
    f = np.float32
    L = nl
    pv = np.zeros((L, 128, NPV), f)

    def fm(v, n):
        return np.ascontiguousarray(np.asarray(v, f).reshape(n, 128).T)
    for l in range(L):
        pv[l, :, PV_NG:PV_NG + 8] = fm(inp["norm_g"][l], 8)
        pv[l, :, PV_MU:PV_MU + 17] = fm(inp["mu_shift"][l], 17)
        pv[l, :, PV_W0:PV_W0 + 4] = fm(inp["w0"][l], 4)
        pv[l, :, PV_A0:PV_A0 + 4] = fm(inp["a0"][l], 4)
        pv[l, :, PV_KK:PV_KK + 4] = fm(inp["k_k"][l], 4)
        pv[l, :, PV_KA:PV_KA + 4] = fm(inp["k_a"][l], 4)
        pv[l, :, PV_RK:PV_RK + 4] = fm(np.asarray(inp["r_k"][l]).reshape(512), 4)
        pv[l, :, PV_LNG:PV_LNG + 4] = fm(inp["lnx_g"][l], 4)
        pv[l, :, PV_LNB:PV_LNB + 4] = fm(inp["lnx_b"][l], 4)
        pv[l, :, PV_QG] = np.tile(np.asarray(inp["q_norm_g"][l], f), 2)
        pv[l, :, PV_KG] = np.tile(np.asarray(inp["k_norm_g"][l], f), 2)
        pv[l, :, PV_BADA:PV_BADA + 24] = fm(inp["b_ada"][l], 24)
        pv[l, :, PV_CB:PV_CB + 8] = np.broadcast_to(np.asarray(inp["rel_bias"][l], f)[:, 191][None, :], (128, 8))
    w2a2 = np.ascontiguousarray(np.concatenate([np.asarray(inp["w2"], f)[:L], np.asarray(inp["a2"], f)[:L]], axis=1))
    p = np.arange(128)[:, None]
    q = np.arange(128)[None, :]
    idx3 = np.clip(128 + q - p, -63, 128) + 63
    idx4 = np.clip(q - p, -63, 128) + 63
    idx = np.concatenate([idx3, idx4], axis=1)
    rb = np.asarray(inp["rel_bias"], f)[:L]
    biasT = np.ascontiguousarray(rb[:, :, idx].transpose(0, 2, 1, 3)).reshape(L, 128, 8 * 256)
    return pv, w2a2, biasT


_CACHE = {}


def _get_prog(nl):
    if nl not in _CACHE:
        _CACHE[nl] = build_program(nl)
    return _CACHE[nl]


def kernel(x, c, norm_g, w_ada, b_ada, w_in, mu_shift, w0, w2, a0, a2, k_k, k_a, r_k, lnx_g, lnx_b, q_norm_g,
           k_norm_g, rel_bias, w_out):
    inp = dict(norm_g=norm_g, b_ada=b_ada, mu_shift=mu_shift, w0=w0, w2=w2, a0=a0, a2=a2, k_k=k_k, k_a=k_a, r_k=r_k,
               lnx_g=lnx_g, lnx_b=lnx_b, q_norm_g=q_norm_g, k_norm_g=k_norm_g, rel_bias=rel_bias)
    f = np.float32
    x = np.asarray(x, f)
    c = np.asarray(c, f)
    pv, w2a2, biasT = _host_layout(inp, NLAYER)
    w_ada = np.ascontiguousarray(np.asarray(w_ada, f))
    w_in = np.ascontiguousarray(np.asarray(w_in, f))
    w_out = np.ascontiguousarray(np.asarray(w_out, f))
    nc = _get_prog(NLAYER)
    in_maps = []
    for i in range(8):
        in_maps.append({
            "xT": np.ascontiguousarray(x[i].T),
            "cfm": np.ascontiguousarray(c[i].reshape(8, 128).T),
            "pv": pv, "w2a2": w2a2, "biasT": biasT,
            "w_ada": w_ada, "w_in": w_in, "w_out": w_out,
        })
    res = run_bass_kernel_spmd(nc, in_maps, core_ids=list(range(8)))
    out = np.stack([np.asarray(r["oT"], f).T for r in res.results], axis=0)
    return np.ascontiguousarray(out)
```

```python
import os as _os
import numpy as np
from contextlib import ExitStack
import concourse.bass as bass
import concourse.mybir as mybir
from concourse.bass_utils import run_bass_kernel_spmd

F32 = mybir.dt.float32
F32R = mybir.dt.float32r
BF16 = mybir.dt.bfloat16
AF = mybir.ActivationFunctionType
ALU = mybir.AluOpType

D = 1024
SEQ = 2048
NLAYER = 4
TB = 512
CH = 128
NC = TB // CH
NBLK = SEQ // TB
C0 = float(np.exp(-0.5))
RMS_EPS = 1e-6
GN_EPS = 64e-5
EPOCH = 8000

PV_NG, PV_MU, PV_W0, PV_A0, PV_KK, PV_KA, PV_RK, PV_LNG, PV_LNB, PV_QG, PV_KG, PV_BADA, PV_CB = (
    0, 8, 25, 29, 33, 37, 41, 45, 49, 53, 54, 55, 79)
NPV = 87


class Res:
    __slots__ = ("name", "lw", "rd", "excl")

    def __init__(self, name, excl=False):
        self.name = name
        self.lw = None
        self.rd = {}
        self.excl = excl


class V:
    __slots__ = ("ap", "res")

    def __init__(self, ap, res):
        self.ap = ap
        self.res = res

    def __getitem__(self, k):
        return V(self.ap[k], self.res)

    def re(self, pat, **kw):
        return V(self.ap.rearrange(pat, **kw), self.res)

    def bc(self, shape):
        return V(self.ap.broadcast_to(shape), self.res)

    def us(self, ax):
        return V(self.ap.unsqueeze(ax), self.res)


class Prog:
    ENGS = ("pe", "dve", "act", "pool", "sp")

    def __init__(self, nc, stack):
        self.nc = nc
        self.stack = stack
        self.ops = {e: [] for e in self.ENGS}
        self.cnt = {}
        self.sems = {}
        self.waited = {e: {} for e in self.ENGS}
        self.pend = {e: ([], []) for e in self.ENGS}

    def _sem_for(self, key, c):
        lst = self.sems.setdefault(key, [])
        idx = (c - 1) // EPOCH
        while len(lst) <= idx:
            lst.append(self.stack.enter_context(self.nc.semaphore("s_%s_%d" % (key.replace(":", "_"), len(lst)))))
        return lst[idx], (c - 1) % EPOCH + 1

    def _add_wait(self, eng, key, c):
        w = self.waited[eng]
        if w.get(key, 0) >= c:
            return
        w[key] = c
        mult = 16 if key.startswith("dma:") else 1
        sem, val = self._sem_for(key, c)
        self.ops[eng].append(lambda e, sem=sem, val=val * mult: e.wait_ge(sem, val))

    def _deps(self, eng, reads, writes):
        deps = {}

        def add(k, c, raw):
            if k == eng and eng == "pe":
                return
            if deps.get(k, 0) < c:
                deps[k] = c
        for r in reads:
            if r.lw is not None:
                add(r.lw[0], r.lw[1], True)
            if r.excl:
                for k, c in r.rd.items():
                    if k != eng:
                        add(k, c, False)
        for w in writes:
            if w.lw is not None:
                add(w.lw[0], w.lw[1], False)
            for k, c in w.rd.items():
                add(k, c, False)
        for k, c in deps.items():
            self._add_wait(eng, k, c)

    def op(self, eng, fn, reads=(), writes=(), signal=True):
        self._deps(eng, reads, writes)
        pr, pw = self.pend[eng]
        pr.extend(reads)
        pw.extend(writes)
        if not signal:
            self.ops[eng].append(lambda e, fn=fn: fn(e))
            return
        c = self.cnt.get(eng, 0) + 1
        self.cnt[eng] = c
        sem, _ = self._sem_for(eng, c)
        self.ops[eng].append(lambda e, fn=fn, sem=sem: fn(e).then_inc(sem, 1))
        for r in pr:
            r.rd[eng] = c
        for w in pw:
            w.lw = (eng, c)
            w.rd = {}
        self.pend[eng] = ([], [])

    def dma(self, queue, slot, fn, reads=(), writes=()):
        key = "dma:" + slot
        self._deps(queue, reads, writes)
        c = self.cnt.get(key, 0) + 1
        self.cnt[key] = c
        sem, _ = self._sem_for(key, c)
        self.ops[queue].append(lambda e, fn=fn, sem=sem: fn(e).then_inc(sem, 16))
        for r in reads:
            r.rd[key] = c
        for w in writes:
            w.lw = (key, c)
            w.rd = {}

    def wait_res(self, eng, res):
        if res.lw is not None:
            self._add_wait(eng, res.lw[0], res.lw[1])

    def run(self):
        with self.nc.Block() as block:
            @block.tensor
            def _(e):
                for f in self.ops["pe"]:
                    f(e)

            @block.vector
            def _(e):
                for f in self.ops["dve"]:
                    f(e)

            @block.scalar
            def _(e):
                for f in self.ops["act"]:
                    f(e)

            @block.gpsimd
            def _(e):
                for f in self.ops["pool"]:
                    f(e)

            @block.sync
            def _(e):
                for f in self.ops["sp"]:
                    f(e)


def build_program(nlayers, dbg=False, nblk=NBLK, stage=99, PIPE=True, W_G=1, W_P=2):
    nc = bass.Bass("TRN2", target_bir_lowering=False)
    L = nlayers
    d_x = nc.dram_tensor("xT", [D, SEQ], F32, kind="ExternalInput").ap()
    d_c = nc.dram_tensor("cfm", [128, 8], F32, kind="ExternalInput").ap()
    d_pv = nc.dram_tensor("pv", [L, 128, NPV], F32, kind="ExternalInput").ap()
    d_w2a2 = nc.dram_tensor("w2a2", [L, 128, 512], F32, kind="ExternalInput").ap()
    d_bias = nc.dram_tensor("biasT", [L, 128, 8 * 256], F32, kind="ExternalInput").ap()
    d_wada = nc.dram_tensor("w_ada", [L, D, 3 * D], F32, kind="ExternalInput").ap()
    d_win = nc.dram_tensor("w_in", [L, D, 4224], F32, kind="ExternalInput").ap()
    d_wout = nc.dram_tensor("w_out", [L, D, D], F32, kind="ExternalInput").ap()
    d_o = nc.dram_tensor("oT", [D, SEQ], F32, kind="ExternalOutput").ap()
    d_xv = d_x.rearrange("(k p) t -> p k t", p=128)
    d_ov = d_o.rearrange("(k p) t -> p k t", p=128)
    d_dbg = {}

    with ExitStack() as st:
        P = Prog(nc, st)

        def sb(name, shape, dt):
            t = st.enter_context(nc.sbuf_tensor(name, list(shape), dt))
            return V(t[tuple(slice(None) for _ in shape)], Res(name))

        def psum(name, shape, dt):
            t = st.enter_context(nc.psum_tensor(name, list(shape), dt))
            return V(t[tuple(slice(None) for _ in shape)], Res(name, excl=True))

        def rs(*vs):
            out = []
            for v in vs:
                if isinstance(v, V):
                    if isinstance(v.res, (list, tuple)):
                        out.extend(v.res)
                    else:
                        out.append(v.res)
            return out

        def split(v, n):
            rl = [Res("%s_%d" % (v.res.name, i)) for i in range(n)]
            v.res = rl
            return [V(v.ap[:, i], rl[i]) for i in range(n)]

        def apof(v):
            return v.ap if isinstance(v, V) else v

        def mmr(out, lhsT, rhs):
            P.op("pe", lambda e: e.matmul(out.ap, lhsT=lhsT.ap.bitcast(F32R), rhs=rhs.ap.bitcast(F32R)),
                 reads=rs(lhsT, rhs), writes=rs(out))

        def mm(out, lhsT, rhs, start=True, stop=True, sig=True, tp=None):
            kw = {}
            if tp is not None:
                kw["tile_position"] = tp
            P.op("pe", lambda e: e.matmul(out.ap, lhsT=lhsT.ap, rhs=rhs.ap, start=start, stop=stop, **kw),
                 reads=rs(lhsT, rhs), writes=rs(out), signal=sig)

        def tr(out, in_, idn, sig=True):
            P.op("pe", lambda e: e.transpose(out.ap, in_.ap, idn.ap), reads=rs(in_, idn), writes=rs(out), signal=sig)

        def act(out, in_, func, bias=None, scale=None):
            kw = {}
            if bias is not None:
                kw["bias"] = apof(bias)
            if scale is not None:
                kw["scale"] = apof(scale)
            P.op("act", lambda e: e.activation(out=out.ap, in_=in_.ap, func=func, **kw),
                 reads=rs(in_, bias, scale), writes=rs(out))

        def tt(eng, out, in0, in1, op):
            P.op(eng, lambda e: e.tensor_tensor(out=out.ap, in0=in0.ap, in1=in1.ap, op=op),
                 reads=rs(in0, in1), writes=rs(out))

        def ts(eng, out, in0, s1, op0, s2=None, op1=None):
            if op1 is None:
                P.op(eng, lambda e: e.tensor_scalar(out=out.ap, in0=in0.ap, scalar1=apof(s1), scalar2=None, op0=op0),
                     reads=rs(in0, s1), writes=rs(out))
            else:
                P.op(eng, lambda e: e.tensor_scalar(out=out.ap, in0=in0.ap, scalar1=apof(s1), scalar2=apof(s2),
                                                    op0=op0, op1=op1),
                     reads=rs(in0, s1, s2), writes=rs(out))

        def stt(out, in0, scalar, in1, op0, op1):
            P.op("dve", lambda e: e.scalar_tensor_tensor(out=out.ap, in0=in0.ap, scalar=apof(scalar), in1=in1.ap,
                                                         op0=op0, op1=op1),
                 reads=rs(in0, scalar, in1), writes=rs(out))

        def cp(eng, out, in_):
            if eng == "act":
                P.op("act", lambda e: e.copy(out=out.ap, in_=in_.ap), reads=rs(in_), writes=rs(out))
            else:
                P.op(eng, lambda e: e.tensor_copy(out=out.ap, in_=in_.ap), reads=rs(in_), writes=rs(out))

        def mset(eng, out, val):
            P.op(eng, lambda e: e.memset(out.ap, val), writes=rs(out))

        def asel(out, pattern, cmp, fill, base, cm):
            P.op("pool", lambda e: e.affine_select(out=out.ap, in_=out.ap, pattern=pattern, compare_op=cmp, fill=fill,
                                                   base=base, channel_multiplier=cm), reads=rs(out), writes=rs(out))

        def dma(queue, slot, out, in_, reads=(), writes=()):
            P.dma(queue, slot, lambda e: e.dma_start(out=apof(out), in_=apof(in_)),
                  reads=list(reads) + rs(in_), writes=list(writes) + rs(out))

        pa = psum("pa", [128, 512], F32)
        pb = psum("pb", [128, 512], F32)
        pm = psum("pm", [128, 512], F32)
        ptr = psum("ptr", [128, 1024], BF16)
        pw = [psum("pw%d" % i, [128, 512], F32) for i in range(4)]

        ident = sb("ident", [128, 128], BF16)
        onesbf = sb("onesbf", [128, 128], BF16)
        onesbd = sb("onesbd", [128, 128], F32)
        onesbdb = sb("onesbdb", [128, 128], BF16)
        sqb = [sb("sqb%d" % i, [128, TB], BF16) for i in range(3)]
        mskscan = sb("mskscan", [128, 512], F32)
        mk4 = sb("mk4", [CH, 4 * CH], F32)
        mksl = sb("mksl", [CH, 2 * CH], F32)
        mset("pool", ident, 0.0)
        asel(ident, [[-1, 128]], ALU.not_equal, 1.0, 0, 1)
        mset("pool", onesbf, 1.0)
        mset("pool", onesbd, 0.0)
        mset("pool", onesbd[0:64, 0:64], 1.0)
        mset("pool", onesbd[64:128, 64:128], 1.0)
        cp("pool", onesbdb, onesbd)
        mset("pool", mskscan, 1.0)
        mset("pool", mskscan.re("p (c t) -> p c t", t=CH)[:, :, 0:1], 0.0)
        mset("pool", mk4, 1.0)
        for i in range(4):
            asel(mk4[:, i * CH:(i + 1) * CH], [[1, CH]], ALU.is_ge, 0.0, -1 if i % 2 == 0 else 0, -1)
        mset("pool", mksl, 1.0)
        for i in range(2):
            asel(mksl[:, i * CH:(i + 1) * CH], [[-1, CH]], ALU.is_ge, 0.0, -1, 1)

        cfm = sb("cfm_s", [128, 8], F32)
        cact = sb("cact", [128, 8], F32)
        cactb = sb("cactb", [128, 8], BF16)
        pvs = sb("pvs", [128, NPV], F32)
        omm = sb("omm", [128, 17], F32)
        mod = sb("mod", [128, 24], F32)
        g1 = sb("g1", [128, 8], F32)
        w2a2b = sb("w2a2b", [128, 512], BF16)
        biasT = sb("biasT_s", [128, 8, 256], F32)
        carry = sb("carry", [128, 17], F32)

        hT = sb("hT", [128, 8, TB], BF16)
        TA = [sb("TA%d" % i, [128, TB], F32) for i in range(8)]
        sq = [sb("sq%d" % i, [128, TB], BF16) for i in range(2)]
        NW = 4
        wst = [sb("wst%d" % i, [128, 8, 128], F32) for i in range(NW)]
        wbf = [sb("wbf%d" % i, [128, 8, 128], BF16) for i in range(NW)]
        Pb = [sb("Pb%d" % i, [128, TB + 4], F32) for i in range(2)]
        la = sb("la", [128, TB], BF16)
        rkvg = [sb("rkvg%d" % i, [128, TB], F32) for i in range(4)]
        tb16 = [sb("tb16_%d" % i, [128, TB], BF16) for i in range(4)]

        At = sb("At", [128, 2, TB], BF16)
        ARbd = sb("ARbd", [128, 2, NC, 4 * CH], BF16)
        Bbd = sb("Bbd", [128, 2, NC, 2 * CH], BF16)
        Bt = sb("Bt", [128, 2, TB], BF16)
        Kt = sb("Kt", [128, 2, TB], BF16)
        Bh = sb("Bh", [CH, 2, NC, 128], BF16)
        Kh = sb("Kh", [CH, 2, NC, 128], BF16)
        Vt = sb("Vt", [CH, 2, NC, 128], BF16)
        Gab = sb("Gab", [CH, 2, NC, 4 * CH], BF16)
        Gk = sb("Gk", [CH, 2, NC, 4 * CH], BF16)
        Aab = sb("Aab", [CH, NC, 2, CH], BF16)
        Pp = [sb("Pp%d" % i, [CH, NC, 2, CH], BF16) for i in range(2)]
        Qq = [sb("Qq%d" % i, [CH, NC, 2, CH], BF16) for i in range(2)]
        TT = sb("TT", [CH, 2, NC, 2, CH], BF16)
        gC = sb("gC", [128, 2, NC], F32)
        gR = sb("gR", [128, 2, NC], F32)
        H32 = sb("H32", [128, 4, 128], F32)
        Hbf = sb("Hbf", [128, 4, 128], BF16)
        tmpH = sb("tmpH", [128, 2, 128], F32)
        Xs = sb("Xs", [CH, 2, 128], BF16)
        Us = sb("Us", [CH, 2, 128], BF16)
        Yfm = sb("Yfm", [128, 2, TB], F32)
        BV = sb("BV", [128, 2, TB], F32)
        SG = sb("SG", [128, 2, TB], BF16)

        kTw = [sb("kTw%d" % i, [128, 4, TB], BF16) for i in range(2)]
        Vw = [sb("Vw%d" % i, [128, 4, 8, 65], BF16) for i in range(2)]
        qbd = sb("qbd", [128, 2, TB], BF16)
        sga = sb("sga", [128, TB], BF16)
        vfm = sb("vfm", [128, TB], BF16)
        PTc = sb("PTc", [128, 2, 5, 128], BF16)
        PT = [PTc[:, i] for i in range(2)]
        sb34c = sb("sb34c", [128, 2, 2, 128], F32)
        sb34 = [sb34c[:, i] for i in range(2)]
        rcpc = sb("rcpc", [128, 2], F32)
        osbc = sb("osbc", [128, 2, 65], F32)
        Otm = [sb("Otm%d" % i, [128, 128], BF16) for i in range(2)]
        ymix = sb("ymix", [128, 8, TB], BF16)
        xo = [sb("xo%d" % i, [128, TB], F32) for i in range(2)]
        xd = [Res("xd%d" % b) for b in range(NBLK)]
        GT = [sb("GT%d" % i, [128, TB], F32) for i in range(2)]
        rstd = GT[0]
        laf = rkvg[0]
        At_, ARbd_, Bbd_, Bt_, Kt_ = split(At, 2), split(ARbd, 2), split(Bbd, 2), split(Bt, 2), split(Kt, 2)
        Bh_, Kh_, Vt_ = split(Bh, 2), split(Kh, 2), split(Vt, 2)
        Gab_, Gk_, TT_ = split(Gab, 2), split(Gk, 2), split(TT, 2)
        gC_, gR_ = split(gC, 2), split(gR, 2)
        Yfm_, BV_, SG_ = split(Yfm, 2), split(BV, 2), split(SG, 2)
        H32_, Hbf_ = split(H32, 4), split(Hbf, 4)
        ymix_ = split(ymix, 8)

        mset("pool", ARbd, 0.0)
        mset("pool", Bbd, 0.0)
        mset("pool", qbd, 0.0)
        for i in range(2):
            mset("pool", Vw[i], 1.0)

        wcount = [0]

        class Ring:
            def __init__(self, slots, ahead):
                self.slots, self.ahead, self.i = slots, ahead, 0

            def next(self):
                s_ = self.slots[self.i % len(self.slots)]
                self.i += 1
                return s_
        ring_all = Ring((0, 1, 2, 3), 2)
        ring_a = Ring((0, 1), 1)
        ring_b = Ring((2, 3), 1)

        def load_w(dram_ap, cast=True, ring=None):
            i = (ring or ring_all).next()
            wcount[0] += 1
            if _os.environ.get("KNODMA") and wcount[0] > 200:
                return wbf[i] if cast else wst[i]
            dma("sp", "wst%d" % i, wst[i], dram_ap.rearrange("(k p) n -> p k n", p=128))
            if cast:
                cp("act" if wcount[0] % 2 == 0 else "dve", wbf[i], wst[i])
                return wbf[i]
            return wst[i]

        dma("sp", "cfm", cfm, d_c)
        act(cact, cfm, AF.Silu)
        cp("dve", cactb, cact)

        class _Stop(Exception):
            pass

        import os as _os
        _tgt = _os.environ.get("KSTOP", "")

        def chk(n):
            if stage < n:
                raise _Stop()

        def chk2(*tag):
            if _tgt and _tgt == ",".join(str(t) for t in tag):
                raise _Stop()

        try:
          for l in range(L):
              dma("sp", "pv", pvs, d_pv[l])
              dma("sp", "w2a2", TA[0], d_w2a2[l])
              dma("sp", "biasT", biasT.re("p h j -> p (h j)"), d_bias[l])
              cp("pool", w2a2b, TA[0])
              for h in range(8):
                  mset("pool", biasT[64:128, h, 128:192], -30000.0)
              ts("dve", omm, pvs[:, PV_MU:PV_MU + 17], -1.0, ALU.mult, 1.0, ALU.add)
              mset("pool", carry, 0.0)
              mset("pool", H32, 0.0)
              mset("pool", Hbf, 0.0)
              wq_ = [load_w(d_wada[l][:, ct_ * 128:(ct_ + 1) * 128]) for ct_ in range(2)]
              for ctile in range(24):
                  if ctile + 2 < 24:
                      wq_.append(load_w(d_wada[l][:, (ctile + 2) * 128:(ctile + 3) * 128]))
                  w = wq_[ctile]
                  for k in range(8):
                      mm(pm[:, ctile:ctile + 1], w[:, k, :], cactb[:, k:k + 1], start=(k == 0), stop=(k == 7),
                         sig=(k == 7))
              tt("dve", mod, pm[:, 0:24], pvs[:, PV_BADA:PV_BADA + 24], ALU.add)
              stt(g1, mod[:, 8:16], 1.0, pvs[:, PV_NG:PV_NG + 8], ALU.add, ALU.mult)
              shift = mod[:, 0:8]
              gate = mod[:, 16:24]
              chk(1)

              pcount = [0]

              def make_block(b):
                  t0 = b * TB
                  slot = b % 2
                  src = d_xv if l == 0 else d_ov
                  xbv = [TA[k] for k in range(8)]

                  def colproj(j, w=None):
                      if w is None:
                          w = load_w(d_win[l][:, j * 128:(j + 1) * 128])
                      ps = pa if pcount[0] % 2 == 0 else pb
                      pcount[0] += 1
                      for k in range(8):
                          mm(ps, w[:, k, :], hT[:, k, :], start=(k == 0), stop=(k == 7), sig=(k == 7))
                      return ps

                  def shifted(j, out, w=None):
                      ps = colproj(j, w=w)
                      pbuf = Pb[j % 2]
                      cp("act", pbuf[:, 0:1], carry[:, j:j + 1])
                      act(pbuf[:, 1:TB + 1], ps, AF.Copy, scale=pvs[:, PV_MU + j:PV_MU + j + 1])
                      cp("act", carry[:, j:j + 1], pbuf[:, TB:TB + 1])
                      stt(out, ps, omm[:, j:j + 1], pbuf[:, 0:TB], ALU.mult, ALU.add)

                  def front(ring):
                      for k in range(8):
                          dma("act", "xb%d" % k, TA[k], src[:, k, t0:t0 + TB], reads=[xd[b]])
                      for k in range(8):
                          act(sq[k % 2], xbv[k], AF.Square)
                          mm(pm, onesbf, sq[k % 2], start=(k == 0), stop=(k == 7), sig=True)
                          if k % 2 == 1:
                              yield
                      act(rstd, pm, AF.Ln, bias=RMS_EPS, scale=1.0 / D)
                      act(rstd, rstd, AF.Exp, scale=-0.5)
                      yield
                      for k in range(8):
                          tt("dve", xbv[k], xbv[k], rstd, ALU.mult)
                          act(hT[:, k, :], xbv[k], AF.Identity, bias=shift[:, k:k + 1], scale=g1[:, k:k + 1])
                          yield
                      shifted(16, laf, w=load_w(d_win[l][:, 16 * 128:17 * 128], ring=ring))
                      act(la[0:64, :], laf[0:64, :], AF.Tanh)
                      cp("act", la[64:128, :], laf[64:128, :])
                      yield

                  def prep(hp, ring=None):
                      ring = ring or ring_all
                      oe = "dve"
                      j = hp % 2
                      r_, k_, v_, g_ = rkvg
                      T = TA
                      col = lambda base: pvs[:, base + hp:base + hp + 1]
                      tiles = [hp, 4 + hp, 8 + hp, 12 + hp]
                      ah = ring.ahead
                      wq = [load_w(d_win[l][:, tiles[i_] * 128:(tiles[i_] + 1) * 128], ring=ring) for i_ in range(ah)]
                      for i, dst in enumerate((r_, k_, v_, g_)):
                          if i + ah < 4:
                              wq.append(load_w(d_win[l][:, tiles[i + ah] * 128:(tiles[i + ah] + 1) * 128], ring=ring))
                          shifted(tiles[i], dst, w=wq[i])
                          yield
                      DM = GT[1]
                      S3 = T[1].re("p (c t) -> p c t", t=CH)

                      def path_decay():
                          mm(pm, w2a2b[0:64, hp * 128:(hp + 1) * 128], la[0:64, :])
                          act(T[0], pm, AF.Sigmoid, bias=col(PV_W0))
                          yield
                          P.op("dve", lambda e: e.tensor_tensor_scan(out=T[1].ap, data0=mskscan.ap, data1=T[0].ap,
                                                                     initial=0.0, op0=ALU.mult, op1=ALU.add),
                               reads=rs(mskscan, T[0]), writes=rs(T[1]))
                          yield
                          tt("dve", DM.re("p (c t) -> p c t", t=CH), S3, S3[:, :, CH // 2 - 1:CH // 2].bc([128, NC, CH]),
                             ALU.subtract)
                          yield
                          tt("dve", T[6], DM, T[0], ALU.subtract)
                          yield
                          tt(oe, T[7].re("p (c t) -> p c t", t=CH), S3[:, :, CH - 1:CH].bc([128, NC, CH]), S3,
                             ALU.subtract)
                          yield
                          act(gC_[j], S3[:, :, CH - 1], AF.Exp, scale=-C0)
                          act(gR_[j], S3[:, :, CH // 2 - 1], AF.Exp, scale=-C0)
                          yield
                          act(T[0], DM, AF.Exp, scale=-C0)
                          yield
                          act(T[1], DM, AF.Exp, scale=C0)
                          yield
                          act(T[6], T[6], AF.Exp, scale=-C0)
                          yield
                          act(T[7], T[7], AF.Exp, scale=-C0)
                          yield

                      def path_kk():
                          mm(pm, w2a2b[64:128, hp * 128:(hp + 1) * 128], la[64:128, :], tp=(64, 0))
                          act(T[2], pm, AF.Sigmoid, bias=col(PV_A0))
                          yield
                          act(sqb[0], k_, AF.Square, scale=col(PV_KK))
                          yield
                          mm(pm, onesbdb, sqb[0])
                          act(T[3], pm, AF.Ln, bias=1e-18)
                          yield
                          act(T[3], T[3], AF.Exp, scale=-0.5)
                          yield
                          stt(T[4], k_, col(PV_KK), T[3], ALU.mult, ALU.mult)
                          yield
                          ts("dve", T[3], T[2], -1.0, ALU.add, col(PV_KA), ALU.mult)
                          yield
                          stt(T[3], T[3], 1.0, k_, ALU.add, ALU.mult)
                          yield
                          tt("dve", T[5], T[4], T[2], ALU.mult)
                          yield
                          stt(sqb[0], r_, col(PV_RK), T[3], ALU.mult, ALU.mult)
                          yield
                          mm(pm, onesbdb, sqb[0])
                          tt("dve", BV_[j], pm, v_, ALU.mult)
                          yield
                      ga, gb = path_decay(), path_kk()
                      live = [ga, gb]
                      while live:
                          for g_i in list(live):
                              try:
                                  next(g_i)
                              except StopIteration:
                                  live.remove(g_i)
                          yield
                      stt(At_[j], T[4], -1.0, T[6], ALU.mult, ALU.mult)
                      tt("dve", Bt_[j], T[5], T[1], ALU.mult)
                      yield
                      tt(oe, Kt_[j], T[3], T[1], ALU.mult)
                      A3 = At_[j].re("p (c t) -> p c t", t=CH)
                      B3 = Bt_[j].re("p (c t) -> p c t", t=CH)
                      tt(oe, tb16[0], r_, T[0], ALU.mult)
                      R3 = tb16[0].re("p (c t) -> p c t", t=CH)
                      for h in range(2):
                          pr = slice(h * 64, (h + 1) * 64)
                          cp("dve", ARbd_[j][pr, :, h * 2 * CH:h * 2 * CH + CH], A3[pr])
                          cp("act", ARbd_[j][pr, :, h * 2 * CH + CH:(h + 1) * 2 * CH], R3[pr])
                          cp("act", Bbd_[j][pr, :, h * CH:(h + 1) * CH], B3[pr])
                          yield
                      tt(oe, tb16[3], T[5], T[7], ALU.mult)
                      tt(oe, tb16[1], T[3], T[7], ALU.mult)
                      cp("act", tb16[2], v_)
                      yield
                      for n_, (src_, dst_) in enumerate(((tb16[3], Bh_[j]), (tb16[1], Kh_[j]), (tb16[2], Vt_[j]))):
                          for c in range(NC):
                              tr(ptr[:, c * 128:(c + 1) * 128], src_[:, c * CH:(c + 1) * CH], ident, sig=(c == NC - 1))
                          yield
                          cp("act" if n_ < 2 else "dve", dst_.re("p c n -> p (c n)"), ptr[:, 0:NC * 128])
                      act(SG_[j], g_, AF.Silu)
                      yield

                  def gmat(hp, wide=False):
                      j = hp % 2
                      bP = (pw[2], pw[3])
                      bQ = (pw[0], pw[1]) if wide else bP
                      for c in range(NC):
                          mm(pw[2], Bt_[j][:, c * CH:(c + 1) * CH], ARbd_[j][:, c, :])
                          mm(pw[3], Kt_[j][:, c * CH:(c + 1) * CH], ARbd_[j][:, c, :])
                          yield
                          tt("dve", Gab_[j][:, c, :], pw[2], mk4, ALU.mult)
                          tt("dve", Gk_[j][:, c, :], pw[3], mk4, ALU.mult)
                      for half in range(2):
                          for c in (2 * half, 2 * half + 1):
                              mm(pw[2 + half][:, (c % 2) * 2 * CH:(c % 2 + 1) * 2 * CH], At_[j][:, c * CH:(c + 1) * CH],
                                 Bbd_[j][:, c, :], sig=(c % 2 == 1))
                      yield
                      for half in range(2):
                          tt("dve", Aab[:, 2 * half:2 * half + 2, :, :].re("p c h s -> p c (h s)"),
                             pw[2 + half].re("p (c n) -> p c n", n=2 * CH), mksl.us(1).bc([CH, 2, 2 * CH]), ALU.mult)
                      G5 = Gab_[j].re("p c (h k t) -> p c h k t", h=2, k=2)
                      Q0 = G5[:, :, :, 0, :]
                      TTj = TT_[j]
                      tt("dve", TTj, Q0, ident.us(1).us(1).bc([CH, NC, 2, CH]), ALU.add)
                      Pc, Qc = Aab, Q0
                      yield

                      def grp(half, lhs, rhs_, banks=None):
                          banks = banks or bP
                          for c in (2 * half, 2 * half + 1):
                              for h in range(2):
                                  o = ((c % 2) * 2 + h) * CH
                                  mm(banks[half][:, o:o + CH], lhs[:, c, h, :], rhs_[:, c, h, :],
                                     sig=(c % 2 == 1 and h == 1))
                      NLEV = 6
                      for lev in range(1, NLEV + 1):
                          Pn, Qn = Pp[lev % 2], Qq[lev % 2]
                          grp(0, Qc, Pc)
                          grp(1, Qc, Pc)
                          if wide and lev < NLEV:
                              grp(0, Pc, Qc, bQ)
                              grp(1, Pc, Qc, bQ)
                          yield
                          cp("act", Pn[:, 0:2].re("p c h t -> p (c h t)"), pw[2])
                          cp("act", Pn[:, 2:4].re("p c h t -> p (c h t)"), pw[3])
                          if lev < NLEV:
                              if not wide:
                                  grp(0, Pc, Qc)
                                  grp(1, Pc, Qc)
                                  yield
                              cp("act", Qn[:, 0:2].re("p c h t -> p (c h t)"), bQ[0])
                              cp("dve", Qn[:, 2:4].re("p c h t -> p (c h t)"), bQ[1])
                          grp(0, Pn, TTj)
                          grp(1, Pn, TTj)
                          yield
                          for half in range(2):
                              tt("dve", TTj[:, 2 * half:2 * half + 2].re("p c h t -> p (c h t)"), pw[2 + half],
                                 TTj[:, 2 * half:2 * half + 2].re("p c h t -> p (c h t)"), ALU.add)
                          Pc, Qc = Pn, Qn
                      yield

                  def chain(pas):
                      hs = slice(pas * 2, pas * 2 + 2)
                      psX = pw[2][:, 0:256]
                      psU = pw[2][:, 256:512]
                      psY = pw[3]
                      psH = pw[2][:, 0:256]
                      for c in range(NC):
                          tt("dve", Hbf[:, hs, :], H32[:, hs, :], gR[:, :, c:c + 1].bc([128, 2, 128]), ALU.mult)
                          tt("pool", tmpH, H32[:, hs, :], gC[:, :, c:c + 1].bc([128, 2, 128]), ALU.mult)
                          yield
                          for j in range(2):
                              hp = pas * 2 + j
                              for h in range(2):
                                  mm(psX[:, j * 128 + h * 64:j * 128 + h * 64 + 64],
                                     Gk_[j][:, c, h * 2 * CH:h * 2 * CH + CH], Vt_[j][:, c, h * 64:(h + 1) * 64],
                                     start=(h == 0), stop=False, sig=False)
                              mm(psX[:, j * 128:(j + 1) * 128], At_[j][:, c * CH:(c + 1) * CH], Hbf_[hp],
                                 start=False, stop=True, sig=(j == 1))
                          yield
                          cp("dve", Xs.re("p j n -> p (j n)"), psX)
                          yield
                          for j in range(2):
                              for h in range(2):
                                  mm(psU[:, j * 128 + h * 64:j * 128 + h * 64 + 64], TT_[j][:, c, h, :],
                                     Xs[:, j, h * 64:(h + 1) * 64], sig=(j == 1 and h == 1))
                          yield
                          cp("act", Us.re("p j n -> p (j n)"), psU)
                          yield
                          for j in range(2):
                              hp = pas * 2 + j
                              Rbd = ARbd_[j][:, c, :].re("p (h k t) -> p h k t", h=2, k=2)[:, :, 1, :]
                              Gb5 = Gab_[j][:, c, :].re("p (h k t) -> p h k t", h=2, k=2)[:, :, 1, :]
                              Gk5 = Gk_[j][:, c, :].re("p (h k t) -> p h k t", h=2, k=2)[:, :, 1, :]
                              mm(psH[:, j * 128:(j + 1) * 128], Bh_[j][:, c, :], Us[:, j, :], start=True, stop=False,
                                 sig=False)
                              mm(psH[:, j * 128:(j + 1) * 128], Kh_[j][:, c, :], Vt_[j][:, c, :], start=False,
                                 stop=True, sig=False)
                              mm(psY[:, j * 2 * CH:(j + 1) * 2 * CH], Hbf_[hp], Rbd, start=True, stop=False, sig=False)
                              mm(psY[:, j * 2 * CH:(j + 1) * 2 * CH], Us[:, j, :], Gb5, start=False, stop=False, sig=False)
                              mm(psY[:, j * 2 * CH:(j + 1) * 2 * CH], Vt_[j][:, c, :], Gk5, start=False, stop=True,
                                 sig=(j == 1))
                          yield
                          for h in range(2):
                              pr = slice(h * 64, (h + 1) * 64)
                              tt("dve", H32[pr, hs, h * 64:(h + 1) * 64],
                                 psH[pr, :].re("p (j n) -> p j n", n=128)[:, :, h * 64:(h + 1) * 64],
                                 tmpH[pr, :, h * 64:(h + 1) * 64], ALU.add)
                          yield
                          for h in range(2):
                              pr = slice(h * 64, (h + 1) * 64)
                              cp("act", Yfm[pr, :, c * CH:(c + 1) * CH],
                                 psY[pr, :].re("p (j n) -> p j n", n=2 * CH)[:, :, h * CH:(h + 1) * CH])
                          yield
                      for j in range(2):
                          hp = pas * 2 + j
                          col = lambda base: pvs[:, base + hp:base + hp + 1]
                          mm(pw[3], onesbd, Yfm_[j])
                          stt(GT[0], pw[3], -1.0 / 64, Yfm_[j], ALU.mult, ALU.add)
                          yield
                          act(sqb[1], GT[0], AF.Square)
                          mm(pw[3], onesbdb, sqb[1])
                          yield
                          act(GT[1], pw[3], AF.Ln, bias=GN_EPS, scale=1.0 / 64)
                          act(GT[1], GT[1], AF.Exp, scale=-0.5)
                          yield
                          tt("dve", GT[0], GT[0], GT[1], ALU.mult)
                          act(GT[0], GT[0], AF.Identity, bias=col(PV_LNB), scale=col(PV_LNG))
                          yield
                          tt("dve", GT[0], GT[0], BV_[j], ALU.add)
                          tt("dve", ymix_[hp], GT[0], SG_[j], ALU.mult)
                          yield

                  def attn(hp):
                      T = TA
                      tiles = [17 + hp, 21 + hp, 25 + hp, 29 + hp]
                      wq = [load_w(d_win[l][:, tiles[i_] * 128:(tiles[i_] + 1) * 128]) for i_ in range(2)]

                      def proj(i):
                          if i + 2 < 4:
                              wq.append(load_w(d_win[l][:, tiles[i + 2] * 128:(tiles[i + 2] + 1) * 128]))
                          return colproj(tiles[i], w=wq[i])
                      for which in range(2):
                          ps = proj(which)
                          yield
                          act(sqb[2], ps, AF.Square)
                          mm(pm, onesbdb, sqb[2])
                          act(T[1], pm, AF.Ln, bias=RMS_EPS, scale=1.0 / 64)
                          act(T[1], T[1], AF.Exp, scale=-0.5)
                          yield
                          gcol = pvs[:, PV_QG + which:PV_QG + which + 1]
                          if which == 0:
                              for h in range(2):
                                  pr = slice(h * 64, (h + 1) * 64)
                                  stt(qbd[pr, h, :], ps[pr, :], gcol[pr, :], T[1][pr, :], ALU.mult, ALU.mult)
                          else:
                              stt(kTw[slot][:, hp, :], ps, gcol, T[1], ALU.mult, ALU.mult)
                          yield
                      ps = proj(2)
                      yield
                      cp("act", vfm, ps)
                      for qt in range(4):
                          tr(ptr[:, qt * 128:(qt + 1) * 128], vfm[:, qt * 128:(qt + 1) * 128], ident, sig=(qt == 3))
                      yield
                      cp("dve", Vw[slot][:, :, 2 * hp:2 * hp + 2, 0:64],
                         ptr[:, 0:512].re("p (q h d) -> p q h d", q=4, h=2))
                      ps = proj(3)
                      yield
                      act(sga, ps, AF.Silu)
                      sbank = ((pw[0], pw[1]), (pa, pb))
                      for qt in range(4):
                          kts = [kt for kt in range(5) if t0 + qt * 128 - 512 + kt * 128 >= 0]
                          locs = {}
                          for kt in kts:
                              tok = qt * 128 - 512 + kt * 128
                              locs[kt] = (slot, tok) if tok >= 0 else (1 - slot, TB + tok)
                          lo = [kt for kt in kts if kt < 3]
                          hi0 = 3 if 3 in kts else 4
                          for h in range(2):
                              b0, b1 = sbank[h]
                              for kt in kts:
                                  ksl, kof = locs[kt]
                                  dst = b0[:, kt * 128:(kt + 1) * 128] if kt < 3 else b1[:, (kt - 3) * 128:(kt - 2) * 128]
                                  mm(dst, kTw[ksl][:, hp, kof:kof + 128], qbd[:, h, qt * 128:(qt + 1) * 128],
                                     sig=(kt >= 3))
                          yield
                          for h in range(2):
                              b0, b1 = sbank[h]
                              head = 2 * hp + h
                              cb = pvs[:, PV_CB + head:PV_CB + head + 1]
                              if lo:
                                  act(PT[h][:, lo[0]:3, :], b0[:, lo[0] * 128:384].re("p (k q) -> p k q", q=128),
                                      AF.Exp, bias=cb, scale=0.125)
                              if 3 in kts:
                                  stt(sb34[h].re("p k q -> p (k q)"), b1[:, 0:256], 0.125,
                                      biasT[:, head, :], ALU.mult, ALU.add)
                              else:
                                  stt(sb34[h][:, 1, :], b1[:, 128:256], 0.125, biasT[:, head, 128:256], ALU.mult, ALU.add)
                          if 0 in lo:
                              mset("dve", PTc[0:64, :, 0, 64:128], 0.0)
                          yield
                          act(PTc[:, :, hi0:5, :], sb34c[:, :, hi0 - 3:2, :], AF.Exp)
                          yield
                          for h in range(2):
                              head = 2 * hp + h
                              psO = pm[:, h * 128:h * 128 + 65]
                              for n, kt in enumerate(kts):
                                  ksl, kof = locs[kt]
                                  mm(psO, PT[h][:, kt, :], Vw[ksl][:, kof // 128, head, :],
                                     start=(n == 0), stop=(n == len(kts) - 1), sig=(n == len(kts) - 1))
                          yield
                          cp("act", osbc, pm[:, 0:256].re("p (h n) -> p h n", n=128)[:, :, 0:65])
                          P.op("dve", lambda e: e.reciprocal(out=rcpc.ap, in_=osbc.ap[:, :, 64]),
                               reads=rs(osbc), writes=rs(rcpc))
                          yield
                          tt("dve", Otm[qt % 2].re("p (h d) -> p h d", h=2), osbc[:, :, 0:64],
                             rcpc.us(2).bc([128, 2, 64]), ALU.mult)
                          yield
                          tr(ptr[:, 512 + (qt % 2) * 128:512 + (qt % 2 + 1) * 128], Otm[qt % 2], ident)
                          yield
                          tt("dve", ymix_[4 + hp][:, qt * 128:(qt + 1) * 128],
                             ptr[:, 512 + (qt % 2) * 128:512 + (qt % 2 + 1) * 128], sga[:, qt * 128:(qt + 1) * 128],
                             ALU.mult)
                          yield

                  def seq(*gens):
                      for g in gens:
                          yield from g

                  def interleave(*gens):
                      gens = [g if isinstance(g, tuple) else (g, 1) for g in gens]
                      while gens:
                          for item in list(gens):
                              g, w_ = item
                              for _ in range(w_):
                                  try:
                                      next(g)
                                  except StopIteration:
                                      gens.remove(item)
                                      break

                  def mid():
                      for pas in range(2):
                          if pas > 0:
                              interleave(prep(2 * pas))
                          interleave((gmat(2 * pas), W_G), (prep(2 * pas + 1), W_P))
                          interleave(gmat(2 * pas + 1), attn(2 * pas))
                          interleave(chain(pas), attn(2 * pas + 1))

                  def outp(ring):
                      ah = ring.ahead
                      wq = [load_w(d_wout[l][:, jj * 128:(jj + 1) * 128], ring=ring) for jj in range(ah)]
                      for j in range(8):
                          if j + ah < 8:
                              wq.append(load_w(d_wout[l][:, (j + ah) * 128:(j + ah + 1) * 128], ring=ring))
                          w = wq[j]
                          ps = pa if pcount[0] % 2 == 0 else pb
                          pcount[0] += 1
                          dma("act", "xo_in%d" % (j % 2), xo[j % 2], src[:, j, t0:t0 + TB], reads=[xd[b]])
                          for k in range(8):
                              mm(ps, w[:, k, :], ymix_[k], start=(k == 0), stop=(k == 7), sig=(k == 7))
                          stt(xo[j % 2], ps, gate[:, j:j + 1], xo[j % 2], ALU.mult, ALU.add)
                          dma("act", "xo%d" % (j % 2), d_ov[:, j, t0:t0 + TB], xo[j % 2], writes=[xd[b]])
                          yield

                  return front, prep, mid, outp, interleave, seq

              blocks = [make_block(b) for b in range(nblk)]
              f0, p0, _, _, interleave, seq = blocks[0]
              interleave(seq(f0(ring_all), p0(0)))
              for b in range(nblk):
                  _, _, mid_b, out_b, _, _ = blocks[b]
                  mid_b()
                  if b + 1 < nblk:
                      f1, p1, _, _, _, _ = blocks[b + 1]
                      interleave(out_b(ring_b), seq(f1(ring_a), p1(0, ring_a)))
                  else:
                      interleave(out_b(ring_all))

        except _Stop:
            pass

        for name, (v, shape) in d_dbg.items():
            dt = nc.dram_tensor("dbg_" + name, shape, F32, kind="ExternalOutput").ap()
            P.dma("sp", "dbg_" + name, (lambda e, v=v, dt=dt: e.dma_start(out=dt, in_=v.ap)), reads=rs(v))
        for key in list(P.cnt.keys()):
            if key.startswith("dma:xo") or key.startswith("dma:dbg"):
                P._add_wait("sp", key, P.cnt[key])
        P.run()
    return nc


def _host_layout(inp, nl):
    f = np.float32
    L = nl
    pv = np.zeros((L, 128, NPV), f)

    def fm(v, n):
        return np.ascontiguousarray(np.asarray(v, f).reshape(n, 128).T)
    for l in range(L):
        pv[l, :, PV_NG:PV_NG + 8] = fm(inp["norm_g"][l], 8)
        pv[l, :, PV_MU:PV_MU + 17] = fm(inp["mu_shift"][l], 17)
        pv[l, :, PV_W0:PV_W0 + 4] = fm(inp["w0"][l], 4)
        pv[l, :, PV_A0:PV_A0 + 4] = fm(inp["a0"][l], 4)
        pv[l, :, PV_KK:PV_KK + 4] = fm(inp["k_k"][l], 4)
        pv[l, :, PV_KA:PV_KA + 4] = fm(inp["k_a"][l], 4)
        pv[l, :, PV_RK:PV_RK + 4] = fm(np.asarray(inp["r_k"][l]).reshape(512), 4)
        pv[l, :, PV_LNG:PV_LNG + 4] = fm(inp["lnx_g"][l], 4)
        pv[l, :, PV_LNB:PV_LNB + 4] = fm(inp["lnx_b"][l], 4)
        pv[l, :, PV_QG] = np.tile(np.asarray(inp["q_norm_g"][l], f), 2)
        pv[l, :, PV_KG] = np.tile(np.asarray(inp["k_norm_g"][l], f), 2)
        pv[l, :, PV_BADA:PV_BADA + 24] = fm(inp["b_ada"][l], 24)
        pv[l, :, PV_CB:PV_CB + 8] = np.broadcast_to(np.asarray(inp["rel_bias"][l], f)[:, 191][None, :], (128, 8))
    w2a2 = np.ascontiguousarray(np.concatenate([np.asarray(inp["w2"], f)[:L], np.asarray(inp["a2"], f)[:L]], axis=1))
    p = np.arange(128)[:, None]
    q = np.arange(128)[None, :]
    idx3 = np.clip(128 + q - p, -63, 128) + 63
    idx4 = np.clip(q - p, -63, 128) + 63
    idx = np.concatenate([idx3, idx4], axis=1)
    rb = np.asarray(inp["rel_bias"], f)[:L]
    biasT = np.ascontiguousarray(rb[:, :, idx].transpose(0, 2, 1, 3)).reshape(L, 128, 8 * 256)
    return pv, w2a2, biasT


_CACHE = {}


def _get_prog(nl):
    if nl not in _CACHE:
        _CACHE[nl] = build_program(nl)
    return _CACHE[nl]


def kernel(x, c, norm_g, w_ada, b_ada, w_in, mu_shift, w0, w2, a0, a2, k_k, k_a, r_k, lnx_g, lnx_b, q_norm_g,
           k_norm_g, rel_bias, w_out):
    inp = dict(norm_g=norm_g, b_ada=b_ada, mu_shift=mu_shift, w0=w0, w2=w2, a0=a0, a2=a2, k_k=k_k, k_a=k_a, r_k=r_k,
               lnx_g=lnx_g, lnx_b=lnx_b, q_norm_g=q_norm_g, k_norm_g=k_norm_g, rel_bias=rel_bias)
    f = np.float32
    x = np.asarray(x, f)
    c = np.asarray(c, f)
    pv, w2a2, biasT = _host_layout(inp, NLAYER)
    w_ada = np.ascontiguousarray(np.asarray(w_ada, f))
    w_in = np.ascontiguousarray(np.asarray(w_in, f))
    w_out = np.ascontiguousarray(np.asarray(w_out, f))
    nc = _get_prog(NLAYER)
    in_maps = []
    for i in range(8):
        in_maps.append({
            "xT": np.ascontiguousarray(x[i].T),
            "cfm": np.ascontiguousarray(c[i].reshape(8, 128).T),
            "pv": pv, "w2a2": w2a2, "biasT": biasT,
            "w_ada": w_ada, "w_in": w_in, "w_out": w_out,
        })
    res = run_bass_kernel_spmd(nc, in_maps, core_ids=list(range(8)))
    out = np.stack([np.asarray(r["oT"], f).T for r in res.results], axis=0)
    return np.ascontiguousarray(out)
```
